# Optimizing a Trainium2 kernel written in Bass

```python
import jax, jax.numpy as jnp
from jax import lax
import numpy as np

D_MODEL = 2048
BATCH = 4
SEQ = 8192
DEPTH = 1

PLE_DIM = 256
CONV_GROUPS = 8
CONV_GROUP_DIM = 128
CONV_CH = CONV_GROUPS * CONV_GROUP_DIM
CONV_WIDTH = 3
N_HEADS = 8
QK_NOPE = 128
QK_ROPE = 64
V_HEAD = 128
Q_LORA = 512
KV_LORA = 256
ATTN_CH = N_HEADS * V_HEAD
MIX_WIDTH = CONV_CH + ATTN_CH
ROPE_THETA = 10000.0
Q_BLOCK = 128
PEER_HEADS = 8
PEER_TOPK = 16
N_KEYS = 128
N_EXPERTS = N_KEYS * N_KEYS
D_QUERY = 256
D_HALF = D_QUERY // 2
PEER_BLOCK = 32
EPS = 1e-6
IN_COLS = 3 * CONV_CH + Q_LORA + KV_LORA + QK_ROPE
IN_SPLITS = [CONV_CH, 2 * CONV_CH, 3 * CONV_CH, 3 * CONV_CH + Q_LORA, 3 * CONV_CH + Q_LORA + KV_LORA]

kernel_name = 'hybrid_shortconv_mla_peer_block'


def rmsnorm(x, g):
    xf = x.astype(jnp.float32)
    y = xf * lax.rsqrt(jnp.mean(xf * xf, axis=-1, keepdims=True) + EPS)
    return (y * g.astype(jnp.float32)).astype(x.dtype)


def head_rmsnorm(y, g, n_groups):
    B, S, C = y.shape
    yf = y.reshape(B, S, n_groups, C // n_groups).astype(jnp.float32)
    yf = yf * lax.rsqrt(jnp.mean(yf * yf, axis=-1, keepdims=True) + EPS)
    return (yf.reshape(B, S, C) * g.astype(jnp.float32)).astype(y.dtype)


def rope_tables(positions):
    inv_freq = ROPE_THETA ** (-(jnp.arange(0, QK_ROPE, 2, dtype=jnp.float32) / QK_ROPE))
    ang = positions.astype(jnp.float32)[..., None] * inv_freq
    return jnp.cos(ang), jnp.sin(ang)


def apply_rope(x, cos, sin):
    x1, x2 = jnp.split(x.astype(jnp.float32), 2, axis=-1)
    return jnp.concatenate([x1 * cos - x2 * sin, x2 * cos + x1 * sin], axis=-1).astype(x.dtype)


def short_conv(xin, b_gate, c_gate, w_conv):
    u = c_gate * xin
    S = u.shape[1]
    u1 = jnp.pad(u, ((0, 0), (1, 0), (0, 0)))[:, :S]
    u2 = jnp.pad(u, ((0, 0), (2, 0), (0, 0)))[:, :S]
    y = w_conv[2] * u + w_conv[1] * u1 + w_conv[0] * u2
    return b_gate * y


def causal_attention(q, k, v):
    B, S, H, Dq = q.shape
    nb = S // Q_BLOCK
    scale = Dq ** -0.5
    qb = q.reshape(B, nb, Q_BLOCK, H, Dq).swapaxes(0, 1)
    kpos = jnp.arange(S)

    def block(args):
        i, qi = args
        s = jnp.einsum('bqhd,bkhd->bhqk', qi, k, preferred_element_type=jnp.float32) * scale
        qpos = i * Q_BLOCK + jnp.arange(Q_BLOCK)
        mask = kpos[None, :] <= qpos[:, None]
        s = jnp.where(mask, s, jnp.finfo(jnp.float32).min)
        pr = jax.nn.softmax(s, axis=-1)
        return jnp.einsum('bhqk,bkhd->bqhd', pr.astype(v.dtype), v)

    out = lax.map(block, (jnp.arange(nb), qb))
    return out.swapaxes(0, 1).reshape(B, S, H * V_HEAD)


def peer(x, w_pq, sub_keys, u_tab, v_tab):
    B, S, D = x.shape
    q = (x @ w_pq).reshape(B, S, PEER_HEADS, 2, D_HALF)
    s = jnp.einsum('bshpd,hpnd->bshpn', q, sub_keys, preferred_element_type=jnp.float32)
    top_v, top_i = lax.top_k(s, PEER_TOPK)
    cand = top_v[..., 0, :, None] + top_v[..., 1, None, :]
    cand = cand.reshape(B, S, PEER_HEADS, PEER_TOPK * PEER_TOPK)
    best, pos = lax.top_k(cand, PEER_TOPK)
    i1 = jnp.take_along_axis(top_i[..., 0, :], pos // PEER_TOPK, axis=-1)
    i2 = jnp.take_along_axis(top_i[..., 1, :], pos % PEER_TOPK, axis=-1)
    expert = i1 * N_KEYS + i2
    gate = jax.nn.softmax(best, axis=-1)

    nb = S // PEER_BLOCK

    def to_blocks(a):
        return a.reshape(B, nb, PEER_BLOCK, *a.shape[2:]).swapaxes(0, 1)

    def block(args):
        xb, eb, gb = args
        u = u_tab[eb]
        hdn = jnp.einsum('bthkd,btd->bthk', u, xb, preferred_element_type=jnp.float32)
        w = (gb * jax.nn.gelu(hdn, approximate=False)).astype(xb.dtype)
        return jnp.einsum('bthk,bthkd->btd', w, v_tab[eb])

    out = lax.map(block, (to_blocks(x), to_blocks(expert), to_blocks(gate)))
    return out.swapaxes(0, 1).reshape(B, S, D)


def setup_inputs(seed: int = 0) -> dict:
    key = jax.random.key(seed)
    ks = jax.random.split(key, 24)
    f32 = jnp.float32

    def nrm(k, shape, scale):
        return jax.random.normal(k, shape, f32) * scale

    def gain(k, shape):
        return 1.0 + 0.02 * jax.random.normal(k, shape, f32)

    x = jax.random.normal(ks[0], (BATCH, SEQ, D_MODEL), f32)
    p = jax.random.normal(ks[1], (DEPTH, BATCH, SEQ, PLE_DIM), f32)
    positions = jnp.broadcast_to(jnp.arange(SEQ, dtype=jnp.int32)[None, :], (BATCH, SEQ))
    return {
        'x': x,
        'p': p,
        'positions': positions,
        'attn_norm': gain(ks[2], (DEPTH, D_MODEL)),
        'w_in': nrm(ks[3], (DEPTH, D_MODEL, IN_COLS), D_MODEL ** -0.5),
        'conv_w': nrm(ks[4], (DEPTH, CONV_WIDTH, CONV_CH), CONV_WIDTH ** -0.5),
        'q_norm': gain(ks[5], (DEPTH, Q_LORA)),
        'w_uq': nrm(ks[6], (DEPTH, Q_LORA, N_HEADS * (QK_NOPE + QK_ROPE)), Q_LORA ** -0.5),
        'kv_norm': gain(ks[7], (DEPTH, KV_LORA)),
        'w_ukv': nrm(ks[8], (DEPTH, KV_LORA, N_HEADS * (QK_NOPE + V_HEAD)), KV_LORA ** -0.5),
        'conv_out_norm': gain(ks[9], (DEPTH, CONV_CH)),
        'attn_out_norm': gain(ks[10], (DEPTH, ATTN_CH)),
        'w_out': nrm(ks[11], (DEPTH, MIX_WIDTH, D_MODEL), MIX_WIDTH ** -0.5),
        'ffn_norm': gain(ks[12], (DEPTH, D_MODEL)),
        'w_pq': nrm(ks[13], (DEPTH, D_MODEL, PEER_HEADS * D_QUERY), D_MODEL ** -0.5),
        'sub_keys': nrm(ks[14], (DEPTH, PEER_HEADS, 2, N_KEYS, D_HALF), D_HALF ** -0.5),
        'u_tab': nrm(ks[15], (DEPTH, N_EXPERTS, D_MODEL), D_MODEL ** -0.5),
        'v_tab': nrm(ks[16], (DEPTH, N_EXPERTS, D_MODEL), PEER_TOPK ** -0.5),
        'ple_norm': gain(ks[17], (DEPTH, D_MODEL)),
        'w_ple_gate': nrm(ks[18], (DEPTH, D_MODEL, D_MODEL), D_MODEL ** -0.5),
        'w_ple_proj': nrm(ks[19], (DEPTH, PLE_DIM, D_MODEL), PLE_DIM ** -0.5),
        'final_norm': gain(ks[20], (D_MODEL,)),
    }


def reference(x, p, positions, attn_norm, w_in, conv_w, q_norm, w_uq, kv_norm, w_ukv,
              conv_out_norm, attn_out_norm, w_out, ffn_norm, w_pq, sub_keys, u_tab, v_tab,
              ple_norm, w_ple_gate, w_ple_proj, final_norm):
    B, S, _ = x.shape
    cos, sin = rope_tables(positions)
    h = x
    for i in range(DEPTH):
        a = rmsnorm(h, attn_norm[i])
        proj = a @ w_in[i]
        xin, b_g, c_g, c_q, c_kv, k_r = jnp.split(proj, IN_SPLITS, axis=-1)

        conv_o = short_conv(xin, b_g, c_g, conv_w[i])

        q = (rmsnorm(c_q, q_norm[i]) @ w_uq[i]).reshape(B, S, N_HEADS, QK_NOPE + QK_ROPE)
        q_nope, q_rope = q[..., :QK_NOPE], q[..., QK_NOPE:]
        kv = (rmsnorm(c_kv, kv_norm[i]) @ w_ukv[i]).reshape(B, S, N_HEADS, QK_NOPE + V_HEAD)
        k_nope, v = kv[..., :QK_NOPE], kv[..., QK_NOPE:]
        q_rope = apply_rope(q_rope, cos[:, :, None, :], sin[:, :, None, :])
        k_r = apply_rope(k_r, cos, sin)
        qf = jnp.concatenate([q_nope, q_rope], axis=-1)
        kf = jnp.concatenate([k_nope, jnp.broadcast_to(k_r[:, :, None, :], (B, S, N_HEADS, QK_ROPE))], axis=-1)
        attn_o = causal_attention(qf, kf, v)

        mixed = jnp.concatenate([head_rmsnorm(conv_o, conv_out_norm[i], CONV_GROUPS),
                                 head_rmsnorm(attn_o, attn_out_norm[i], N_HEADS)], axis=-1)
        h = h + mixed @ w_out[i]

        f = rmsnorm(h, ffn_norm[i])
        h = h + peer(f, w_pq[i], sub_keys[i], u_tab[i], v_tab[i])

        gate = jax.nn.sigmoid((rmsnorm(h, ple_norm[i]) @ w_ple_gate[i]).astype(jnp.float32))
        h = h + ((p[i] @ w_ple_proj[i]).astype(jnp.float32) * gate).astype(h.dtype)
    return rmsnorm(h, final_norm)
```

```python
import numpy as np
from contextlib import ExitStack
import concourse.bass as bass
import concourse.mybir as mybir
from concourse.bass_utils import run_bass_kernel_spmd

F32 = mybir.dt.float32; BF16 = mybir.dt.bfloat16; I32 = mybir.dt.int32; U32 = mybir.dt.uint32
AF = mybir.ActivationFunctionType; ALU = mybir.AluOpType; AX = mybir.AxisListType

D = 2048; NCH = 16
EPS = 1e-6
SCALE = 192.0 ** -0.5
N_CORES = 8


class Ev:
    __slots__ = ('sem', 'val', 'eng', 'dma')
    def __init__(s, sem, val, eng, dma): s.sem = sem; s.val = val; s.eng = eng; s.dma = dma


class Res:
    def __init__(s, name='r'): s.name = name; s.w = None; s.r = {}


class DSem:
    def __init__(s, nc, name): s.sem = nc.alloc_semaphore(name); s.val = 0


class KB:
    def __init__(self, nc):
        self.nc = nc
        self.E = {'pe': nc.tensor, 'act': nc.scalar, 'dve': nc.vector, 'pool': nc.gpsimd, 'sp': nc.sync}
        self.sem = {e: nc.alloc_semaphore('s_' + e) for e in ('pe', 'act', 'dve', 'pool')}
        self.cnt = {e: 0 for e in self.sem}
        self.waited = {e: {} for e in self.E}
        self.dsems = []
        self.dsem_by = {}
    def dsem(self, name):
        if name in self.dsem_by: return self.dsem_by[name]
        d = DSem(self.nc, name); self.dsems.append(d); self.dsem_by[name] = d; return d
    def _wait(self, eng, ev):
        kk = ev.sem.num
        if self.waited[eng].get(kk, 0) < ev.val:
            self.E[eng].wait_ge(ev.sem, ev.val)
            self.waited[eng][kk] = ev.val
    def op(self, eng, fn, reads=(), writes=(), dsem=None):
        deps = []
        for r in reads:
            if r.w is not None: deps.append(r.w)
        for w in writes:
            if w.w is not None and (w.w.dma or w.w.eng != eng or eng != 'pe'): deps.append(w.w)
            for ev in w.r.values():
                deps.append(ev)
        for ev in deps: self._wait(eng, ev)
        ins = fn()
        if dsem is not None:
            dsem.val += 16; ins.then_inc(dsem.sem, 16); ev = Ev(dsem.sem, dsem.val, eng, True)
        else:
            self.cnt[eng] += 1; ins.then_inc(self.sem[eng], 1); ev = Ev(self.sem[eng], self.cnt[eng], eng, False)
        for r in reads:
            r.r[(ev.sem.num)] = ev
        for w in writes:
            w.w = ev; w.r = {}
        return ev
    def barrier(self):
        for e in self.E:
            for o in self.sem:
                if o != e and self.cnt[o] > 0:
                    self._wait(e, Ev(self.sem[o], self.cnt[o], o, False))
            for d in self.dsems:
                if d.val > 0: self._wait(e, Ev(d.sem, d.val, 'x', True))


def build(NBH, phases=6, dbg=False):
    TP = NBH * 512; NK = 2 * TP; NQ = TP; NKT = NK // 128; NKTP = TP // 128; NQB = NQ // 512
    nc = bass.Bass("TRN2", target_bir_lowering=False)
    k = KB(nc)
    def din(name, shape, dt=F32): return nc.dram_tensor(name, shape, dt, kind="ExternalInput").ap()
    skind = "ExternalOutput" if dbg else "Internal"
    def dscr(name, shape, dt): return nc.dram_tensor(name, shape, dt, kind=skind).ap()
    xp = din("xp", [TP, D]); xo = din("xo", [TP, D]); pp = din("pp", [TP, 256])
    posr = din("posr", [1, NK], I32); kbias = din("kbias", [1, NK])
    cst_d = din("cst", [128, 128]); ident_d = din("ident", [128, 128])
    w_in = din("w_in", [D, 3904]); w_in_sw = din("w_in_sw", [D, 64])
    w_uq = din("w_uq", [512, 1536]); w_uq_sw = din("w_uq_sw", [512, 512]); w_ukv = din("w_ukv", [256, 2048])
    w_out = din("w_out", [D, D]); w_pq = din("w_pq", [D, D]); w_pg = din("w_pg", [D, D]); w_pp = din("w_pp", [256, D])
    skT_d = din("skT", [128, 2048]); gffn_d = din("gffn_rep", [128, D]); gfin_d = din("gfin_rep", [128, D])
    if phases >= 5:
        u_tab = din("u_tab", [16384, D]); v_tab = din("v_tab", [16384, D])
    out_d = nc.dram_tensor("out", [TP, D], F32, kind="ExternalOutput").ap()
    ckvn_s = dscr("ckvn_s", [2, 128, NK], BF16); kr_s = dscr("kr_s", [64, NK], BF16)
    cqn_s = dscr("cqn_s", [4, 128, NQ], BF16); cs_s = dscr("cs_s", [64, NQ], F32); sn_s = dscr("sn_s", [64, NQ], F32)
    mix_s = dscr("mix_s", [16, 128, NQ], BF16); h_s = dscr("h_s", [TP, D], F32)
    uv16 = nc.dram_tensor("uv16", [16384, 2 * D], BF16, kind="Internal").ap()
    def d16(name, shape): return nc.dram_tensor(name, shape, BF16, kind="Internal").ap()
    wc16 = d16("wc16", [D, 3072]); wq16 = d16("wq16", [512, 2048]); wkv16 = d16("wkv16", [256, 2048])
    wo16 = d16("wo16", [D, D]); wpq16 = d16("wpq16", [D, D]); wpg16 = d16("wpg16", [D, D]); wpp16 = d16("wpp16", [256, D]); sk16 = d16("sk16", [128, D])

    C_INVF, C_SSC, C_GATTN, C_GQ, C_GKV, C_GCO, C_GAO, C_GFFN, C_GPLE, C_CW = 0, 1, 2, 18, 22, 24, 32, 40, 56, 72

    es_all = ExitStack()
    uid = [0]
    def SB(es, name, shape, dt):
        uid[0] += 1
        return es.enter_context(nc.sbuf_tensor(f"sb{uid[0]}_{name}", shape, dt))
    def PS(es, name, shape, dt):
        uid[0] += 1
        return es.enter_context(nc.psum_tensor(f"ps{uid[0]}_{name}", shape, dt))
    V = nc.vector; A = nc.scalar; P = nc.gpsimd; T = nc.tensor; SP = nc.sync

    with nc.allow_low_precision("bf16 matmul operands, fp32 accumulation"), es_all:
        cst = SB(es_all, "cst", [128, 128], F32); Rcst = Res()
        identb = SB(es_all, "identb", [128, 128], BF16); ones32 = SB(es_all, "ones32", [128, 128], F32)
        onesb = SB(es_all, "onesb", [128, 128], BF16); epsc = SB(es_all, "epsc", [128, 1], F32); tri = SB(es_all, "tri", [128, 128], BF16)
        Rconst = Res()
        wst = []
        def alloc_wst(es_):
            wst.clear()
            for i in range(3):
                wst.append((SB(es_, f"wst{i}", [128, 1024], F32), Res(), k.dsem(f"dw{i}")))
        es_w0 = ExitStack(); alloc_wst(es_w0)
        d_c = k.dsem("d_c")
        k.op('sp', lambda: SP.dma_start(out=cst[:], in_=cst_d[:, :]), writes=[Rcst], dsem=d_c)
        k.op('sp', lambda: SP.dma_start(out=wst[0][0][:, 0:128], in_=ident_d[:, :]), writes=[wst[0][1]], dsem=wst[0][2])
        k.op('dve', lambda: V.tensor_copy(out=identb[:], in_=wst[0][0][:, 0:128]), reads=[wst[0][1]], writes=[Rconst])
        k.op('dve', lambda: V.memset(ones32[:], 1.0), writes=[Rconst])
        k.op('dve', lambda: V.memset(onesb[:], 1.0), writes=[Rconst])
        k.op('dve', lambda: V.memset(epsc[:], EPS), writes=[Rconst])
        k.op('pool', lambda: P.memset(tri[:], 1.0), writes=[Rconst])
        k.op('pool', lambda: P.affine_select(out=tri[:], in_=tri[:], pattern=[[1, 128]], compare_op=ALU.is_ge, fill=0.0,
                                             base=0, channel_multiplier=-1), reads=[Rconst], writes=[Rconst])
        wcnt = [0]
        def load_weight(src, nchunks, col0, ncols, dst, dres, dcol0):
            for c in range(nchunks):
                for cc in range(0, ncols, 1024):
                    w = min(1024, ncols - cc)
                    slot = wcnt[0] % 3; wcnt[0] += 1
                    st, rst, ds = wst[slot]
                    k.op('sp', lambda: SP.dma_start(out=st[:, 0:w], in_=src[c * 128:(c + 1) * 128, col0 + cc:col0 + cc + w]),
                         writes=[rst], dsem=ds)
                    o = dst[:, c, dcol0 + cc:dcol0 + cc + w]
                    if slot == 0: k.op('act', lambda: A.copy(out=o, in_=st[:, 0:w]), reads=[rst], writes=[dres[0]])
                    elif slot == 1: k.op('pool', lambda: P.tensor_copy(out=o, in_=st[:, 0:w]), reads=[rst], writes=[dres[1]])
                    else: k.op('dve', lambda: V.tensor_copy(out=o, in_=st[:, 0:w]), reads=[rst], writes=[dres[2]])

        def rstd_from_ss(es_unused, ss_ap, n, out_ap, tmp_ap, Rin, Rtmp, Rout):
            k.op('act', lambda: A.activation(out=tmp_ap, in_=ss_ap, func=AF.Sqrt, scale=1.0 / n, bias=epsc[0:tmp_ap.shape[0], 0:1]),
                 reads=[Rin, Rconst], writes=[Rtmp])
            k.op('dve', lambda: V.reciprocal(out=out_ap, in_=tmp_ap), reads=[Rtmp], writes=[Rout])

        class XA:
            pass
        def make_xa(es):
            xa = XA()
            xa.xst = [(SB(es, f"xst{i}", [128, D], F32), Res(), k.dsem(f"dx{i}")) for i in range(2)]
            xa.xs = SB(es, "xs", [128, D], BF16); xa.Rxs = Res()
            xa.st = SB(es, "xstat", [128, 4], F32); xa.Rst = Res()
            xa.psT = PS(es, "psT", [128, D], BF16); xa.RpsT = Res()
            xa.cnt = 0
            return xa
        def make_aT(xa, src, row0, ntiles, gcol, aT, RaT):
            for j in range(ntiles):
                xt, Rx, dx = xa.xst[xa.cnt % 2]; xa.cnt += 1
                k.op('sp', lambda: SP.dma_start(out=xt[:], in_=src[row0 + j * 128: row0 + (j + 1) * 128, :]), writes=[Rx], dsem=dx)
                k.op('act', lambda: A.activation(out=xa.xs[:], in_=xt[:], func=AF.Square, accum_out=xa.st[:, 0:1]),
                     reads=[Rx], writes=[xa.Rxs, xa.Rst])
                rstd_from_ss(None, xa.st[:, 0:1], D, xa.st[:, 2:3], xa.st[:, 1:2], xa.Rst, xa.Rst, xa.Rst)
                k.op('dve', lambda: V.tensor_scalar(out=xa.xs[:], in0=xt[:], scalar1=xa.st[:, 2:3], scalar2=None, op0=ALU.mult),
                     reads=[Rx, xa.Rst], writes=[xa.Rxs])
                def tr():
                    ins = None
                    for c in range(NCH):
                        ins = T.transpose(xa.psT[:, c * 128:(c + 1) * 128], xa.xs[:, c * 128:(c + 1) * 128], identb[:])
                    return ins
                k.op('pe', tr, reads=[xa.Rxs, Rconst], writes=[xa.RpsT])
                k.op('dve', lambda: V.tensor_tensor(out=aT[:, :, j * 128:(j + 1) * 128],
                                                    in0=xa.psT[:].rearrange("p (c t) -> p c t", c=NCH),
                                                    in1=cst[:, gcol:gcol + NCH].unsqueeze(2).to_broadcast([128, NCH, 128]),
                                                    op=ALU.mult), reads=[xa.RpsT, Rcst], writes=[RaT])

        k.barrier(); es_w0.close()
        R16 = {}
        cast_jobs = []
        def cast_w(name, dst, src, nrows, col0, ncols, dcol0=0):
            ds = k.dsem("dpw_" + name); r = R16.setdefault(name, Res())
            for r0_ in range(0, nrows, 256):
                n = min(256, nrows - r0_)
                cast_jobs.append((lambda dst=dst, src=src, r0_=r0_, n=n, dcol0=dcol0, ncols=ncols, col0=col0:
                                  P.dma_start(out=dst[r0_:r0_ + n, dcol0:dcol0 + ncols], in_=src[r0_:r0_ + n, col0:col0 + ncols]), r, ds))
        cast_w("wc", wc16, w_in, D, 0, 3072)
        cast_w("wq", wq16, w_uq, 512, 0, 1536); cast_w("wq", wq16, w_uq_sw, 512, 0, 512, 1536)
        cast_w("wkv", wkv16, w_ukv, 256, 0, 2048)
        cast_w("wo", wo16, w_out, D, 0, D)
        if phases >= 5:
            cast_w("wpq", wpq16, w_pq, D, 0, D); cast_w("sk", sk16, skT_d, 128, 0, D)
        if phases >= 6:
            cast_w("wpg", wpg16, w_pg, D, 0, D); cast_w("wpp", wpp16, w_pp, 256, 0, D)
        def emit_casts(nmax):
            for _ in range(nmax):
                if not cast_jobs: return
                fn, r, ds = cast_jobs.pop(0)
                k.op('pool', fn, writes=[r], dsem=ds)
        def load_w16(name, src16, nchunks, ncols, dst, dres):
            ds = k.dsem("dlw_" + name)
            for c in range(nchunks):
                k.op('sp', lambda: SP.dma_start(out=dst[:, c, 0:ncols], in_=src16[c * 128:(c + 1) * 128, 0:ncols]), reads=[R16[name]], writes=[dres[0]], dsem=ds)
        with ExitStack() as es:
            alloc_wst(es)
            wl = SB(es, "wl", [128, NCH, 896], BF16); Rwl = [Res(), Res(), Res()]
            load_weight(w_in, NCH, 3072, 832, wl, Rwl, 0)
            load_weight(w_in_sw, NCH, 0, 64, wl, Rwl, 832)
            xa = make_xa(es)
            aT = SB(es, "aT", [128, NCH, 512], BF16); RaT = Res()
            posi = SB(es, "posi", [64, 512], I32); Rposi = Res(); dpos = k.dsem("dpos")
            ang = SB(es, "ang", [64, 512], F32); Rang = Res()
            r1 = SB(es, "r1", [64, 512], F32); Rr1 = Res()
            ki = SB(es, "ki", [64, 512], I32); Rki = Res()
            kf = SB(es, "kf", [64, 512], F32); Rkf = Res()
            cs = SB(es, "cs", [64, 512], F32); Rcs = Res()
            sn = SB(es, "sn", [64, 512], F32); Rsn = Res()
            lat = [SB(es, f"lat{i}", [128, 512], F32) for i in range(4)]; Rlat = [Res() for _ in range(4)]
            sq = SB(es, "sq", [128, 512], F32); Rsq = Res()
            sd = SB(es, "sd", [128, 512], F32); Rsd = Res()
            rs = SB(es, "rs", [128, 512], F32); Rrs = Res()
            latb = [SB(es, f"latb{i}", [128, 512], BF16) for i in range(2)]; Rlatb = [Res(), Res()]; dlat = [k.dsem("dlat0"), k.dsem("dlat1")]
            t1 = SB(es, "t1", [64, 512], F32); Rt1 = Res()
            t2 = SB(es, "t2", [64, 512], F32); Rt2 = Res()
            krb = SB(es, "krb", [64, 512], BF16); Rkrb = Res(); dkr = k.dsem("dkr")
            dcs = k.dsem("dcs")
            psl = [PS(es, f"psl{i}", [128, 512], F32) for i in range(4)]; Rpsl = [Res() for _ in range(4)]
            pss = PS(es, "pss", [128, 512], F32); Rpss = Res()
            lcnt = [0]

            def normed_group(nchunk, wcol0, gcol, nfeat, dst_s, col_off):
                for j in range(nchunk):
                    def mm(j=j):
                        ins = None
                        for c in range(NCH):
                            ins = T.matmul(psl[j][:], wl[:, c, wcol0 + j * 128: wcol0 + (j + 1) * 128], aT[:, c, :],
                                           start=(c == 0), stop=(c == NCH - 1))
                        return ins
                    k.op('pe', mm, reads=[RaT] + Rwl, writes=[Rpsl[j]])
                    k.op('act', lambda j=j: A.copy(out=lat[j][:], in_=psl[j][:]), reads=[Rpsl[j]], writes=[Rlat[j]])
                    k.op('act', lambda j=j: A.activation(out=sq[:], in_=lat[j][:], func=AF.Square), reads=[Rlat[j]], writes=[Rsq])
                    k.op('pe', lambda j=j: T.matmul(pss[:], ones32[:], sq[:], start=(j == 0), stop=(j == nchunk - 1)),
                         reads=[Rsq, Rconst], writes=[Rpss])
                rstd_from_ss(None, pss[:], nfeat, rs[:], sd[:], Rpss, Rsd, Rrs)
                for j in range(nchunk):
                    lb, Rlb, dl = latb[lcnt[0] % 2], Rlatb[lcnt[0] % 2], dlat[lcnt[0] % 2]; lcnt[0] += 1
                    k.op('dve', lambda j=j: V.scalar_tensor_tensor(out=lb[:], in0=lat[j][:], scalar=cst[:, gcol + j:gcol + j + 1], in1=rs[:],
                                                                  op0=ALU.mult, op1=ALU.mult), reads=[Rlat[j], Rrs, Rcst], writes=[Rlb])
                    k.op('sp', lambda j=j: SP.dma_start(out=dst_s[j, :, col_off:col_off + 512], in_=lb[:]), reads=[Rlb], dsem=dl)

            aTs = [(aT, RaT), (SB(es, "aT_b", [128, NCH, 512], BF16), Res())]
            css = [(cs, Rcs, sn, Rsn), (SB(es, "cs_b", [64, 512], F32), Res(), SB(es, "sn_b", [64, 512], F32), Res())]
            def front1a(blk):
                nonlocal aT, RaT, cs, Rcs, sn, Rsn
                aT, RaT = aTs[blk % 2]; cs, Rcs, sn, Rsn = css[blk % 2]
                own = blk >= NBH
                src = xo if own else xp
                make_aT(xa, src, (blk % NBH) * 512, 4, C_GATTN, aT, RaT)
                k.op('sp', lambda: SP.dma_start(out=posi[:], in_=posr[:, blk * 512:(blk + 1) * 512].partition_broadcast(64)),
                     writes=[Rposi], dsem=dpos)
                k.op('dve', lambda: V.tensor_copy(out=ang[:], in_=posi[:]), reads=[Rposi], writes=[Rang])
                k.op('dve', lambda: V.tensor_scalar(out=ang[:], in0=ang[:], scalar1=cst[0:64, C_INVF:C_INVF + 1], scalar2=None, op0=ALU.mult),
                     reads=[Rang, Rcst], writes=[Rang])
                for which in range(2):
                    if which == 0:
                        k.op('dve', lambda: V.tensor_scalar(out=ki[:], in0=ang[:], scalar1=float(1 / (2 * np.pi)), scalar2=None, op0=ALU.mult),
                             reads=[Rang], writes=[Rki])
                        src_ang = ang
                    else:
                        k.op('dve', lambda: V.tensor_scalar(out=r1[:], in0=ang[:], scalar1=float(np.pi / 2), scalar2=None, op0=ALU.add),
                             reads=[Rang], writes=[Rr1])
                        k.op('dve', lambda: V.tensor_scalar(out=ki[:], in0=r1[:], scalar1=float(1 / (2 * np.pi)), scalar2=None, op0=ALU.mult),
                             reads=[Rr1], writes=[Rki])
                        src_ang = r1
                    k.op('dve', lambda: V.tensor_copy(out=kf[:], in_=ki[:]), reads=[Rki], writes=[Rkf])
                    k.op('dve', lambda: V.scalar_tensor_tensor(out=r1[:], in0=kf[:], scalar=-6.28125, in1=src_ang[:], op0=ALU.mult, op1=ALU.add),
                         reads=[Rkf, Rang, Rr1], writes=[Rr1])
                    k.op('dve', lambda: V.scalar_tensor_tensor(out=r1[:], in0=kf[:], scalar=-(2 * np.pi - 6.28125), in1=r1[:], op0=ALU.mult, op1=ALU.add),
                         reads=[Rkf, Rr1], writes=[Rr1])
                    k.op('dve', lambda: V.tensor_scalar(out=r1[:], in0=r1[:], scalar1=3.1415925, scalar2=-3.1415925, op0=ALU.min, op1=ALU.max),
                         reads=[Rr1], writes=[Rr1])
                    if which == 0:
                        k.op('act', lambda: A.activation(out=sn[:], in_=r1[:], func=AF.Sin, scale=cst[0:64, C_SSC:C_SSC + 1]),
                             reads=[Rr1, Rcst], writes=[Rsn])
                    else:
                        k.op('act', lambda: A.activation(out=cs[:], in_=r1[:], func=AF.Sin), reads=[Rr1], writes=[Rcs])
                if own:
                    qoff = (blk - NBH) * 512
                    k.op('sp', lambda: SP.dma_start(out=cs_s[:, qoff:qoff + 512], in_=cs[:]), reads=[Rcs], dsem=dcs)
                    k.op('sp', lambda: SP.dma_start(out=sn_s[:, qoff:qoff + 512], in_=sn[:]), reads=[Rsn], dsem=dcs)
            def back1a(blk):
                nonlocal aT, RaT, cs, Rcs, sn, Rsn
                aT, RaT = aTs[blk % 2]; cs, Rcs, sn, Rsn = css[blk % 2]
                own = blk >= NBH
                normed_group(2, 512, C_GKV, 256, ckvn_s, blk * 512)
                def mmkr(col, j):
                    ins = None
                    for c in range(NCH):
                        ins = T.matmul(psl[j][0:64, :], wl[:, c, col:col + 64], aT[:, c, :], start=(c == 0), stop=(c == NCH - 1))
                    return ins
                k.op('pe', lambda: mmkr(768, 2), reads=[RaT] + Rwl, writes=[Rpsl[2]])
                k.op('pe', lambda: mmkr(832, 3), reads=[RaT] + Rwl, writes=[Rpsl[3]])
                k.op('dve', lambda: V.tensor_tensor(out=t1[:], in0=psl[2][0:64, :], in1=cs[:], op=ALU.mult), reads=[Rpsl[2], Rcs], writes=[Rt1])
                k.op('dve', lambda: V.tensor_tensor(out=t2[:], in0=psl[3][0:64, :], in1=sn[:], op=ALU.mult), reads=[Rpsl[3], Rsn], writes=[Rt2])
                k.op('dve', lambda: V.tensor_tensor(out=krb[:], in0=t1[:], in1=t2[:], op=ALU.add), reads=[Rt1, Rt2], writes=[Rkrb])
                k.op('sp', lambda: SP.dma_start(out=kr_s[:, blk * 512:(blk + 1) * 512], in_=krb[:]), reads=[Rkrb], dsem=dkr)
                if own:
                    normed_group(4, 0, C_GQ, 512, cqn_s, (blk - NBH) * 512)
                emit_casts(1 if (blk >= 2 and blk < 10) else 0)
            front1a(0)
            for blk in range(2 * NBH):
                if blk + 1 < 2 * NBH: front1a(blk + 1)
                back1a(blk)
            emit_casts(8 - min(8, 8 * 1) + (8 if NBH < 5 else 0))
            k.barrier()
        if phases < 2:
            k.op('sp', lambda: SP.dma_start(out=out_d[0:128, :], in_=xo[0:128, :]), dsem=d_c)
            SP.wait_ge(d_c.sem, d_c.val)
            return nc
        with ExitStack() as es:
            alloc_wst(es)
            wc = SB(es, "wc", [128, NCH, 3072], BF16); Rwc = [Res(), Res(), Res()]
            load_w16("wc", wc16, NCH, 3072, wc, Rwc)
            xa = make_xa(es)
            aT = SB(es, "aT", [128, NCH, 512], BF16); RaT = Res()
            carry = SB(es, "carry", [128, 8, 2], F32); Rcarry = Res()
            xsb = SB(es, "xsb", [128, 512], F32); Rxsb = Res()
            ub = SB(es, "ub", [128, 514], F32); Rub = Res()
            yb = SB(es, "yb", [128, 512], F32); Ryb = Res()
            ob = SB(es, "ob", [128, 512], F32); Rob = Res()
            sq = SB(es, "sq", [128, 512], F32); Rsq = Res()
            sd = SB(es, "sd", [128, 512], F32); Rsd = Res()
            rs = SB(es, "rs", [128, 512], F32); Rrs = Res()
            onb = [SB(es, f"onb{i}", [128, 512], BF16) for i in range(2)]; Ronb = [Res(), Res()]; don = [k.dsem("don0"), k.dsem("don1")]
            psx = PS(es, "psx", [128, 512], F32); Rpsx = Res()
            psb = PS(es, "psb", [128, 512], F32); Rpsb = Res()
            psc = PS(es, "psc", [128, 512], F32); Rpsc = Res()
            pss = PS(es, "pss", [128, 512], F32); Rpss = Res()
            def mmc(ps, col, n):
                ins = None
                for c in range(NCH):
                    ins = T.matmul(ps[:, 0:n], wc[:, c, col:col + 128], aT[:, c, 0:n], start=(c == 0), stop=(c == NCH - 1))
                return ins
            make_aT(xa, xp, TP - 128, 1, C_GATTN, aT, RaT)
            for g in range(8):
                k.op('pe', lambda: mmc(psx, g * 128, 128), reads=[RaT] + Rwc, writes=[Rpsx])
                k.op('pe', lambda: mmc(psc, 2048 + g * 128, 128), reads=[RaT] + Rwc, writes=[Rpsc])
                k.op('act', lambda: A.copy(out=xsb[:, 0:128], in_=psx[:, 0:128]), reads=[Rpsx], writes=[Rxsb])
                k.op('dve', lambda: V.tensor_tensor(out=ub[:, 0:128], in0=psc[:, 0:128], in1=xsb[:, 0:128], op=ALU.mult),
                     reads=[Rpsc, Rxsb], writes=[Rub])
                k.op('dve', lambda: V.tensor_copy(out=carry[:, g, :], in_=ub[:, 126:128]), reads=[Rub], writes=[Rcarry])
            ocnt = 0
            for blk in range(NBH):
                emit_casts(5)
                make_aT(xa, xo, blk * 512, 4, C_GATTN, aT, RaT)
                for g in range(8):
                    k.op('pe', lambda: mmc(psx, g * 128, 512), reads=[RaT] + Rwc, writes=[Rpsx])
                    k.op('pe', lambda: mmc(psc, 2048 + g * 128, 512), reads=[RaT] + Rwc, writes=[Rpsc])
                    k.op('pe', lambda: mmc(psb, 1024 + g * 128, 512), reads=[RaT] + Rwc, writes=[Rpsb])
                    k.op('act', lambda: A.copy(out=xsb[:], in_=psx[:]), reads=[Rpsx], writes=[Rxsb])
                    k.op('dve', lambda: V.tensor_copy(out=ub[:, 0:2], in_=carry[:, g, :]), reads=[Rcarry], writes=[Rub])
                    k.op('dve', lambda: V.tensor_tensor(out=ub[:, 2:514], in0=psc[:], in1=xsb[:], op=ALU.mult), reads=[Rpsc, Rxsb], writes=[Rub])
                    cw = C_CW + g * 3
                    k.op('dve', lambda: V.tensor_scalar(out=yb[:], in0=ub[:, 2:514], scalar1=cst[:, cw + 2:cw + 3], scalar2=None, op0=ALU.mult),
                         reads=[Rub, Rcst], writes=[Ryb])
                    k.op('dve', lambda: V.scalar_tensor_tensor(out=yb[:], in0=ub[:, 1:513], scalar=cst[:, cw + 1:cw + 2], in1=yb[:], op0=ALU.mult, op1=ALU.add),
                         reads=[Rub, Rcst, Ryb], writes=[Ryb])
                    k.op('dve', lambda: V.scalar_tensor_tensor(out=yb[:], in0=ub[:, 0:512], scalar=cst[:, cw:cw + 1], in1=yb[:], op0=ALU.mult, op1=ALU.add),
                         reads=[Rub, Rcst, Ryb], writes=[Ryb])
                    k.op('dve', lambda: V.tensor_copy(out=carry[:, g, :], in_=ub[:, 512:514]), reads=[Rub], writes=[Rcarry])
                    k.op('dve', lambda: V.tensor_tensor(out=ob[:], in0=psb[:], in1=yb[:], op=ALU.mult), reads=[Rpsb, Ryb], writes=[Rob])
                    k.op('act', lambda: A.activation(out=sq[:], in_=ob[:], func=AF.Square), reads=[Rob], writes=[Rsq])
                    k.op('pe', lambda: T.matmul(pss[:], ones32[:], sq[:], start=True, stop=True), reads=[Rsq, Rconst], writes=[Rpss])
                    rstd_from_ss(None, pss[:], 128, rs[:], sd[:], Rpss, Rsd, Rrs)
                    on, Ron, dn = onb[ocnt % 2], Ronb[ocnt % 2], don[ocnt % 2]; ocnt += 1
                    k.op('dve', lambda: V.scalar_tensor_tensor(out=on[:], in0=ob[:], scalar=cst[:, C_GCO + g:C_GCO + g + 1], in1=rs[:],
                                                              op0=ALU.mult, op1=ALU.mult), reads=[Rob, Rrs, Rcst], writes=[Ron])
                    k.op('sp', lambda: SP.dma_start(out=mix_s[g, :, blk * 512:(blk + 1) * 512], in_=on[:]), reads=[Ron], dsem=dn)
            emit_casts(1000)
            k.barrier()
        if phases < 3:
            k.op('sp', lambda: SP.dma_start(out=out_d[0:128, :], in_=xo[0:128, :]), dsem=d_c)
            SP.wait_ge(d_c.sem, d_c.val)
            return nc
        with ExitStack() as es:
            wq = SB(es, "wq", [128, 4, 2048], BF16); Rwq = [Res(), Res(), Res()]
            wkv = SB(es, "wkv", [128, 2, 2048], BF16); Rwkv = [Res(), Res(), Res()]
            ckv = SB(es, "ckv", [128, 2, NK], BF16); Rckv = Res(); dl2 = k.dsem("dl2")
            krT = SB(es, "krT", [65, NK], BF16); RkrT = Res()
            cqn = SB(es, "cqn", [128, 4, NQ], BF16); Rcqn = Res()
            es_w2 = ExitStack(); alloc_wst(es_w2)
            load_w16("wq", wq16, 4, 2048, wq, Rwq)
            load_w16("wkv", wkv16, 2, 2048, wkv, Rwkv)
            for j in range(2):
                k.op('sp', lambda: SP.dma_start(out=ckv[:, j, :], in_=ckvn_s[j, :, :]), writes=[Rckv], dsem=dl2)
            for j in range(4):
                k.op('sp', lambda: SP.dma_start(out=cqn[:, j, :], in_=cqn_s[j, :, :]), writes=[Rcqn], dsem=dl2)
            k.op('sp', lambda: SP.dma_start(out=krT[0:64, :], in_=kr_s[:, :]), writes=[RkrT], dsem=dl2)
            for cc in range(0, NK, 1024):
                st, rst, ds = wst[0]
                k.op('sp', lambda: SP.dma_start(out=st[64:65, 0:1024], in_=kbias[:, cc:cc + 1024]), writes=[rst], dsem=ds)
                k.op('dve', lambda: V.tensor_copy(out=krT[64:65, cc:cc + 1024], in_=st[64:65, 0:1024]), reads=[rst], writes=[RkrT])
            k.barrier(); es_w2.close()
            Khs = [(SB(es, f"Kh{i}", [128, NK], BF16), Res()) for i in range(2)]
            Vhs = [(SB(es, f"Vh{i}", [128, NKT, 128], BF16), Res()) for i in range(2)]
            qns = [(SB(es, f"qn{i}", [128, 512], BF16), Res()) for i in range(2)]
            qrs = [(SB(es, f"qr{i}", [65, 512], BF16), Res()) for i in range(2)]
            for i in range(2):
                k.op('dve', lambda: V.memset(qrs[i][0][64:65, :], 1.0), writes=[qrs[i][1]])
            csqs = [(SB(es, f"csq{i}", [64, 512], F32), SB(es, f"snq{i}", [64, 512], F32), Res(), k.dsem(f"dcsq{i}")) for i in range(2)]
            t1 = SB(es, "t1", [64, 512], F32); Rt1 = Res()
            t2 = SB(es, "t2", [64, 512], F32); Rt2 = Res()
            pT = [SB(es, f"pT{i}", [128, 512], BF16) for i in range(3)]; RpT = [Res() for _ in range(3)]
            rden = SB(es, "rden", [128, 512], F32); Rrden = Res()
            daccs = [(SB(es, f"dacc{i}", [128, 512], F32), Res()) for i in range(2)]
            ob = SB(es, "ob", [128, 512], F32); Rob = Res()
            sq = SB(es, "sq", [128, 512], F32); Rsq = Res()
            sd = SB(es, "sd", [128, 512], F32); Rsd = Res()
            rs = SB(es, "rs", [128, 512], F32); Rrs = Res()
            onb = [SB(es, f"onb{i}", [128, 512], BF16) for i in range(2)]; Ronb = [Res(), Res()]; don = [k.dsem("doa0"), k.dsem("doa1")]
            psg = [PS(es, f"psg{i}", [128, 512], F32) for i in range(3)]; Rpsg = [Res() for _ in range(3)]
            pS = [PS(es, f"pS{i}", [128, 512], F32) for i in range(2)]; RpS = [Res(), Res()]
            psos = [(PS(es, f"pso{i}", [128, 512], F32), Res()) for i in range(2)]
            psf = PS(es, "psf", [128, 512], F32); Rpsf = Res()
            gcnt = 0; scnt = 0; pcnt = 0; ocnt = 0
            dpre = [k.dsem(f"dpre{i}") for i in range(4)]
            print("phase2 sbuf bytes remaining", nc.sbuf_bytes_remaining)
            def prepass():
                cc = 0
                for ch in range(64):
                    for which, tab in enumerate((u_tab, v_tab)):
                        k.op('pool', lambda: P.dma_start(out=uv16[ch * 256:(ch + 1) * 256, which * D:(which + 1) * D], in_=tab[ch * 256:(ch + 1) * 256, :]),
                             dsem=dpre[cc % 4])
                        cc += 1
                        yield
            pre_gen = prepass() if phases >= 5 else iter(())
            n_iter_total = 8 * sum(NKTP + 4 * qb + 4 for qb in range(NQB))
            pre_every = max(1, n_iter_total // 136)
            it_cnt = 0

            def kv_gen(h):
                nonlocal gcnt
                Kh, RKh = Khs[h % 2]; Vh, RVh = Vhs[h % 2]
                for n in range(NK // 512):
                    g = gcnt % 3; gcnt += 1
                    def mm():
                        ins = None
                        for c in range(2):
                            ins = T.matmul(psg[g][:], wkv[:, c, h * 256:h * 256 + 128], ckv[:, c, n * 512:(n + 1) * 512], start=(c == 0), stop=(c == 1))
                        return ins
                    k.op('pe', mm, reads=[Rckv] + Rwkv, writes=[Rpsg[g]])
                    if n % 2 == 0: k.op('act', lambda: A.copy(out=Kh[:, n * 512:(n + 1) * 512], in_=psg[g][:]), reads=[Rpsg[g]], writes=[RKh])
                    else: k.op('dve', lambda: V.tensor_copy(out=Kh[:, n * 512:(n + 1) * 512], in_=psg[g][:]), reads=[Rpsg[g]], writes=[RKh])
                    yield
                for n in range(NKT // 4):
                    g = gcnt % 3; gcnt += 1
                    def mm():
                        ins = None
                        for i in range(4):
                            kt = n * 4 + i
                            for c in range(2):
                                ins = T.matmul(psg[g][:, i * 128:(i + 1) * 128], ckv[:, c, kt * 128:(kt + 1) * 128],
                                               wkv[:, c, h * 256 + 128:h * 256 + 256], start=(c == 0), stop=(c == 1))
                        return ins
                    k.op('pe', mm, reads=[Rckv] + Rwkv, writes=[Rpsg[g]])
                    o = Vh[:, n * 4:(n + 1) * 4, :]
                    i_ = psg[g][:].rearrange("p (a b) -> p a b", a=4)
                    if n % 2 == 0: k.op('act', lambda: A.copy(out=o, in_=i_), reads=[Rpsg[g]], writes=[RVh])
                    else: k.op('dve', lambda: V.tensor_copy(out=o, in_=i_), reads=[Rpsg[g]], writes=[RVh])
                    yield

            def q_gen(h, qb, qi):
                nonlocal gcnt
                q0 = qb * 512
                qn, Rqn = qns[qi]; qr, Rqr = qrs[qi]; csq, snq, Rcsq, dcsq = csqs[qi]
                k.op('sp', lambda: SP.dma_start(out=csq[:], in_=cs_s[:, q0:q0 + 512]), writes=[Rcsq], dsem=dcsq)
                k.op('sp', lambda: SP.dma_start(out=snq[:], in_=sn_s[:, q0:q0 + 512]), writes=[Rcsq], dsem=dcsq)
                ga, gb_, gc = [(gcnt + i) % 3 for i in range(3)]; gcnt += 3
                def mmq(ps, col, m):
                    ins = None
                    for c in range(4):
                        ins = T.matmul(ps[0:m, :], wq[:, c, col:col + m], cqn[:, c, q0:q0 + 512], start=(c == 0), stop=(c == 3))
                    return ins
                k.op('pe', lambda: mmq(psg[ga], h * 192, 128), reads=[Rcqn] + Rwq, writes=[Rpsg[ga]])
                yield
                k.op('pe', lambda: mmq(psg[gb_], h * 192 + 128, 64), reads=[Rcqn] + Rwq, writes=[Rpsg[gb_]])
                k.op('pe', lambda: mmq(psg[gc], 1536 + h * 64, 64), reads=[Rcqn] + Rwq, writes=[Rpsg[gc]])
                k.op('act', lambda: A.activation(out=qn[:], in_=psg[ga][:], func=AF.Copy, scale=SCALE), reads=[Rpsg[ga]], writes=[Rqn])
                yield
                k.op('dve', lambda: V.scalar_tensor_tensor(out=t1[:], in0=psg[gb_][0:64, :], scalar=SCALE, in1=csq[:], op0=ALU.mult, op1=ALU.mult),
                     reads=[Rpsg[gb_], Rcsq], writes=[Rt1])
                k.op('dve', lambda: V.scalar_tensor_tensor(out=t2[:], in0=psg[gc][0:64, :], scalar=SCALE, in1=snq[:], op0=ALU.mult, op1=ALU.mult),
                     reads=[Rpsg[gc], Rcsq], writes=[Rt2])
                yield
                k.op('dve', lambda: V.tensor_tensor(out=qr[0:64, :], in0=t1[:], in1=t2[:], op=ALU.add), reads=[Rt1, Rt2], writes=[Rqr])
                yield

            def finalize(h, qb, oi):
                nonlocal ocnt
                q0 = qb * 512
                pso, Rpso = psos[oi]; dacc, Rdacc = daccs[oi]
                yield
                yield
                k.op('pe', lambda: T.matmul(psf[:], ones32[:], dacc[:], start=True, stop=True), reads=[Rdacc, Rconst], writes=[Rpsf])
                yield
                k.op('dve', lambda: V.reciprocal(out=rden[:], in_=psf[:]), reads=[Rpsf], writes=[Rrden])
                k.op('dve', lambda: V.tensor_tensor(out=ob[:], in0=pso[:], in1=rden[:], op=ALU.mult), reads=[Rpso, Rrden], writes=[Rob])
                k.op('act', lambda: A.activation(out=sq[:], in_=ob[:], func=AF.Square), reads=[Rob], writes=[Rsq])
                yield
                yield
                yield
                k.op('pe', lambda: T.matmul(psf[:], ones32[:], sq[:], start=True, stop=True), reads=[Rsq, Rconst], writes=[Rpsf])
                yield
                rstd_from_ss(None, psf[:], 128, rs[:], sd[:], Rpsf, Rsd, Rrs)
                on, Ron, dn = onb[ocnt % 2], Ronb[ocnt % 2], don[ocnt % 2]; ocnt += 1
                k.op('dve', lambda: V.scalar_tensor_tensor(out=on[:], in0=ob[:], scalar=cst[:, C_GAO + h:C_GAO + h + 1], in1=rs[:],
                                                          op0=ALU.mult, op1=ALU.mult), reads=[Rob, Rrs, Rcst], writes=[Ron])
                k.op('sp', lambda: SP.dma_start(out=mix_s[8 + h, :, q0:q0 + 512], in_=on[:]), reads=[Ron], dsem=dn)
                yield

            def run_all(g):
                for _ in g: pass
            seq = [(h, qb) for h in range(8) for qb in range(NQB)]
            run_all(kv_gen(0)); run_all(q_gen(0, 0, 0))
            fin_gen = iter(())
            for idx, (h, qb) in enumerate(seq):
                Kh, RKh = Khs[h % 2]; Vh, RVh = Vhs[h % 2]
                qn, Rqn = qns[idx % 2]; qr, Rqr = qrs[idx % 2]
                pso, Rpso = psos[idx % 2]; dacc, Rdacc = daccs[idx % 2]
                nxt = seq[idx + 1] if idx + 1 < len(seq) else None
                kvg = kv_gen(nxt[0]) if (nxt is not None and nxt[0] != h) else iter(())
                qg = q_gen(nxt[0], nxt[1], (idx + 1) % 2) if nxt is not None else iter(())
                nkt = NKTP + 4 * qb + 4
                def emit_pv(kt, c0, pi_):
                    k.op('pe', lambda: T.matmul(pso[:, c0:512], Vh[:, kt, :], pT[pi_][:, c0:512], start=(kt == 0), stop=(kt == nkt - 1)),
                         reads=[RVh, RpT[pi_]], writes=[Rpso])
                    if kt == 0:
                        k.op('dve', lambda: V.tensor_copy(out=dacc[:], in_=pT[pi_][:]), reads=[RpT[pi_]], writes=[Rdacc])
                    else:
                        k.op('dve', lambda: V.tensor_tensor(out=dacc[:, c0:512], in0=dacc[:, c0:512], in1=pT[pi_][:, c0:512], op=ALU.add),
                             reads=[RpT[pi_], Rdacc], writes=[Rdacc])
                pend = None
                kv_done = False
                for kt in range(nkt):
                    j = kt - (NKTP + 4 * qb)
                    c0 = 0 if j <= 0 else j * 128
                    s = scnt % 2; scnt += 1
                    pi_ = pcnt % 3; pcnt += 1
                    def mms():
                        T.matmul(pS[s][:, c0:512], Kh[:, kt * 128:(kt + 1) * 128], qn[:, c0:512], start=True, stop=False)
                        return T.matmul(pS[s][:, c0:512], krT[0:65, kt * 128:(kt + 1) * 128], qr[0:65, c0:512], start=False, stop=True)
                    k.op('pe', mms, reads=[RKh, RkrT, Rqn, Rqr], writes=[RpS[s]])
                    k.op('act', lambda: A.activation(out=pT[pi_][:, c0:512], in_=pS[s][:, c0:512], func=AF.Exp), reads=[RpS[s]], writes=[RpT[pi_]])
                    if j >= 0:
                        k.op('pool', lambda: P.tensor_tensor(out=pT[pi_][:, c0:c0 + 128], in0=pT[pi_][:, c0:c0 + 128], in1=tri[:], op=ALU.mult),
                             reads=[RpT[pi_], Rconst], writes=[RpT[pi_]])
                    if pend is not None: emit_pv(*pend)
                    pend = (kt, c0, pi_)
                    it_cnt += 1
                    if it_cnt % pre_every == 0: next(pre_gen, None)
                    next(fin_gen, None)
                    if kt >= 2 and not kv_done:
                        if next(kvg, 'done') == 'done': kv_done = True
                    if kv_done and kt >= nkt - 6: next(qg, None)
                emit_pv(*pend)
                run_all(fin_gen); run_all(kvg); run_all(qg)
                fin_gen = finalize(h, qb, idx % 2)
            run_all(fin_gen)
            for _ in pre_gen: pass
            k.barrier()
        if phases < 4:
            k.op('sp', lambda: SP.dma_start(out=out_d[0:128, :], in_=xo[0:128, :]), dsem=d_c)
            SP.wait_ge(d_c.sem, d_c.val)
            return nc
        with ExitStack() as es:
            alloc_wst(es)
            wo = SB(es, "wo", [128, NCH, D], BF16); Rwo = [Res(), Res(), Res()]
            load_w16("wo", wo16, NCH, D, wo, Rwo)
            mT = [SB(es, f"mT{i}", [128, NCH, 512], BF16) for i in range(2)]; RmT = [Res(), Res()]; dmT = [k.dsem("dmT0"), k.dsem("dmT1")]
            xst = [(SB(es, f"x3_{i}", [128, D], F32), Res(), k.dsem(f"dx{i}")) for i in range(2)]
            h1 = [(SB(es, f"h1_{i}", [128, D], F32), Res(), k.dsem(f"dh{i}")) for i in range(2)]
            psw = [PS(es, f"psw{i}", [128, 512], F32) for i in range(4)]; Rpsw = [Res() for _ in range(4)]
            tcnt = 0; pcnt = 0
            for blk in range(NBH):
                m, Rm, dm = mT[blk % 2], RmT[blk % 2], dmT[blk % 2]
                for half in range(2):
                    k.op('sp', lambda: SP.dma_start(out=m[:, half * 8:(half + 1) * 8, :],
                                                    in_=mix_s[half * 8:(half + 1) * 8, :, blk * 512:(blk + 1) * 512].rearrange("g p t -> p g t")),
                         writes=[Rm], dsem=dm)
                for tt in range(4):
                    xt, Rx, dx = xst[tcnt % 2]; ht, Rh, dh = h1[tcnt % 2]; tcnt += 1
                    r0 = blk * 512 + tt * 128
                    k.op('sp', lambda: SP.dma_start(out=xt[:], in_=xo[r0:r0 + 128, :]), writes=[Rx], dsem=dx)
                    for cq in range(4):
                        pi_ = pcnt % 4; pcnt += 1
                        def mm():
                            ins = None
                            for c in range(NCH):
                                ins = T.matmul(psw[pi_][:], m[:, c, tt * 128:(tt + 1) * 128], wo[:, c, cq * 512:(cq + 1) * 512],
                                               start=(c == 0), stop=(c == NCH - 1))
                            return ins
                        k.op('pe', mm, reads=[Rm] + Rwo, writes=[Rpsw[pi_]])
                        k.op('dve', lambda: V.tensor_tensor(out=ht[:, cq * 512:(cq + 1) * 512], in0=psw[pi_][:], in1=xt[:, cq * 512:(cq + 1) * 512], op=ALU.add),
                             reads=[Rpsw[pi_], Rx], writes=[Rh])
                    k.op('sp', lambda: SP.dma_start(out=h_s[r0:r0 + 128, :], in_=ht[:]), reads=[Rh], dsem=dh)
            k.barrier()
        if phases < 5:
            k.op('sp', lambda: SP.dma_start(out=out_d[0:128, :], in_=xo[0:128, :]), dsem=d_c)
            SP.wait_ge(d_c.sem, d_c.val)
            return nc
        NT = TP // 128
        with ExitStack() as es:
            wpq = SB(es, "wpq", [128, NCH, D], BF16); Rwpq = [Res(), Res(), Res()]
            skb = SB(es, "skb", [128, D], BF16); Rskb = [Res(), Res(), Res()]
            gffn = SB(es, "gffn", [128, D], F32); Rgffn = Res(); dg = k.dsem("dg")
            es_w4 = ExitStack(); alloc_wst(es_w4)
            load_w16("wpq", wpq16, NCH, D, wpq, Rwpq)
            load_w16("sk", sk16, 1, D, skb[:].rearrange("p (o n) -> p o n", o=1), Rskb)
            k.op('sp', lambda: SP.dma_start(out=gffn[:], in_=gffn_d[:, :]), writes=[Rgffn], dsem=dg)
            k.barrier(); es_w4.close()
            NBUF = 7
            gb = [(SB(es, f"gb{i}", [128, 2 * D], BF16), Res(), k.dsem(f"dgb{i}")) for i in range(NBUF)]
            hts = [(SB(es, f"ht{i}", [128, D], F32), Res(), k.dsem(f"dh{i}")) for i in range(2)]; dst_ = k.dsem("dhst")
            fbs = [(SB(es, f"fb{i}", [128, D], BF16), Res()) for i in range(2)]
            fT = SB(es, "fT", [128, NCH, 128], BF16); RfT = Res()
            qT = SB(es, "qT", [128, 16, 128], BF16); RqT = Res()
            sc = SB(es, "sc", [128, 16, 128], F32); Rsc = Res()
            tmp = SB(es, "tmp", [128, 256], F32); Rtmp = Res()
            tv = SB(es, "tv", [128, 16, 16], F32); Rtv = Res()
            ti = SB(es, "ti", [128, 16, 16], U32); Rti = Res()
            tif = SB(es, "tif", [128, 16, 16], F32); Rtif = Res()
            cand = SB(es, "cand", [128, 8, 256], F32); Rcand = Res()
            bv = SB(es, "bv", [128, 8, 16], F32); Rbv = Res()
            bp = SB(es, "bp", [128, 8, 16], U32); Rbp = Res()
            bpf = SB(es, "bpf", [128, 8, 16], F32); Rbpf = Res()
            ai = SB(es, "ai", [128, 8, 16], I32); Rai = Res()
            af = SB(es, "af", [128, 8, 16], F32); Raf = Res()
            bf_ = SB(es, "bf_", [128, 8, 16], F32); Rbf = Res()
            eqa = SB(es, "eqa", [128, 8, 16, 16], BF16); Reqa = Res()
            prod = SB(es, "prod", [128, 8, 16, 16], BF16); Rprod = Res()
            i1 = SB(es, "i1", [128, 8, 16], F32); Ri1 = Res()
            i2 = SB(es, "i2", [128, 8, 16], F32); Ri2 = Res()
            eidf = SB(es, "eidf", [128, 128], F32); Reidf = Res()
            eids = [(SB(es, f"eid{i}", [128, 128], I32), Res()) for i in range(2)]
            gts = [(SB(es, f"gt{i}", [128, 8, 16], F32), Res()) for i in range(2)]
            zz = SB(es, "zz", [128, 16], F32); Rzz = Res()
            hdn = SB(es, "hdn", [128, 128], F32); Rhdn = [Res() for _ in range(128)]
            wv = SB(es, "wv", [128, 128], F32); Rwv = [Res() for _ in range(128)]
            junk = SB(es, "junk", [128, D], BF16); Rjunk = Res()
            dg_ = [(SB(es, f"dg{i}", [128, 128], BF16), Res()) for i in range(3)]
            st4 = SB(es, "st4", [128, 4], F32); Rst4 = Res()
            psT = PS(es, "psT4", [128, D], BF16); RpsT = Res()
            psX = [PS(es, f"psX{i}", [128, 512], F32) for i in range(2)]; RpsX = [Res(), Res()]
            pv = PS(es, "pv", [128, D], F32); Rpv = Res()
            IOTA = 96
            gcnt = 0; xcnt = 0; dcnt = 0
            print("phase4 sbuf bytes remaining", nc.sbuf_bytes_remaining)
            def prep(tt, b):
                nonlocal xcnt
                r0 = tt * 128
                ht, Rht, dht = hts[b]; fb, Rfb = fbs[b]; eid, Reid = eids[b]; gt, Rgt = gts[b]
                k.op('sp', lambda: SP.dma_start(out=ht[:], in_=h_s[r0:r0 + 128, :]), writes=[Rht], dsem=dht)
                k.op('act', lambda: A.activation(out=fb[:], in_=ht[:], func=AF.Square, accum_out=st4[:, 0:1]), reads=[Rht], writes=[Rfb, Rst4])
                rstd_from_ss(None, st4[:, 0:1], D, st4[:, 2:3], st4[:, 1:2], Rst4, Rst4, Rst4)
                k.op('dve', lambda: V.scalar_tensor_tensor(out=fb[:], in0=ht[:], scalar=st4[:, 2:3], in1=gffn[:], op0=ALU.mult, op1=ALU.mult),
                     reads=[Rht, Rst4, Rgffn], writes=[Rfb])
                yield
                def tr():
                    ins = None
                    for c in range(NCH):
                        ins = T.transpose(psT[:, c * 128:(c + 1) * 128], fb[:, c * 128:(c + 1) * 128], identb[:])
                    return ins
                k.op('pe', tr, reads=[Rfb, Rconst], writes=[RpsT])
                k.op('act', lambda: A.copy(out=fT[:].rearrange("p c t -> p (c t)"), in_=psT[:]), reads=[RpsT], writes=[RfT])
                yield
                for q4 in range(4):
                    pq = psX[xcnt % 2]; Rpq = RpsX[xcnt % 2]; xcnt += 1
                    def mm():
                        ins = None
                        for i in range(4):
                            hp = q4 * 4 + i
                            for c in range(NCH):
                                ins = T.matmul(pq[:, i * 128:(i + 1) * 128], wpq[:, c, hp * 128:(hp + 1) * 128], fT[:, c, :],
                                               start=(c == 0), stop=(c == NCH - 1))
                        return ins
                    k.op('pe', mm, reads=[RfT] + Rwpq, writes=[Rpq])
                    o = qT[:, q4 * 4:(q4 + 1) * 4, :].rearrange("p a t -> p (a t)")
                    k.op('act', lambda: A.copy(out=o, in_=pq[:]), reads=[Rpq], writes=[RqT])
                    yield
                yield
                for q4 in range(4):
                    pq = psX[xcnt % 2]; Rpq = RpsX[xcnt % 2]; xcnt += 1
                    def mm():
                        ins = None
                        for i in range(4):
                            hp = q4 * 4 + i
                            ins = T.matmul(pq[:, i * 128:(i + 1) * 128], qT[:, hp, :], skb[:, hp * 128:(hp + 1) * 128], start=True, stop=True)
                        return ins
                    k.op('pe', mm, reads=[RqT] + Rskb, writes=[Rpq])
                    o = sc[:, q4 * 4:(q4 + 1) * 4, :].rearrange("p a t -> p (a t)")
                    k.op('act', lambda: A.copy(out=o, in_=pq[:]), reads=[Rpq], writes=[Rsc])
                    yield
                for hp in range(16):
                    k.op('dve', lambda: V.max(out=tv[:, hp, 0:8], in_=sc[:, hp, :]), reads=[Rsc], writes=[Rtv])
                    k.op('dve', lambda: V.max_index(out=ti[:, hp, 0:8], in_max=tv[:, hp, 0:8], in_values=sc[:, hp, :]), reads=[Rsc, Rtv], writes=[Rti])
                    k.op('dve', lambda: V.match_replace(out=tmp[:, 0:128], in_to_replace=tv[:, hp, 0:8], in_values=sc[:, hp, :], imm_value=-1e30),
                         reads=[Rsc, Rtv], writes=[Rtmp])
                    k.op('dve', lambda: V.max(out=tv[:, hp, 8:16], in_=tmp[:, 0:128]), reads=[Rtmp], writes=[Rtv])
                    k.op('dve', lambda: V.max_index(out=ti[:, hp, 8:16], in_max=tv[:, hp, 8:16], in_values=tmp[:, 0:128]), reads=[Rtmp, Rtv], writes=[Rti])
                    if hp % 2 == 1: yield
                k.op('dve', lambda: V.tensor_copy(out=tif[:], in_=ti[:]), reads=[Rti], writes=[Rtif])
                tvv = tv[:].rearrange("p (h two) k -> p h two k", two=2)
                tfv = tif[:].rearrange("p (h two) k -> p h two k", two=2)
                k.op('dve', lambda: V.tensor_tensor(out=cand[:].rearrange("p h (a b) -> p h a b", a=16),
                                                    in0=tvv[:, :, 0, :].unsqueeze(3).to_broadcast([128, 8, 16, 16]),
                                                    in1=tvv[:, :, 1, :].unsqueeze(2).to_broadcast([128, 8, 16, 16]), op=ALU.add),
                     reads=[Rtv], writes=[Rcand])
                for h in range(8):
                    k.op('dve', lambda: V.max(out=bv[:, h, 0:8], in_=cand[:, h, :]), reads=[Rcand], writes=[Rbv])
                    k.op('dve', lambda: V.max_index(out=bp[:, h, 0:8], in_max=bv[:, h, 0:8], in_values=cand[:, h, :]), reads=[Rcand, Rbv], writes=[Rbp])
                    k.op('dve', lambda: V.match_replace(out=tmp[:], in_to_replace=bv[:, h, 0:8], in_values=cand[:, h, :], imm_value=-1e30),
                         reads=[Rcand, Rbv], writes=[Rtmp])
                    k.op('dve', lambda: V.max(out=bv[:, h, 8:16], in_=tmp[:]), reads=[Rtmp], writes=[Rbv])
                    k.op('dve', lambda: V.max_index(out=bp[:, h, 8:16], in_max=bv[:, h, 8:16], in_values=tmp[:]), reads=[Rtmp, Rbv], writes=[Rbp])
                    if h % 2 == 1: yield
                k.op('dve', lambda: V.tensor_copy(out=bpf[:], in_=bp[:]), reads=[Rbp], writes=[Rbpf])
                k.op('dve', lambda: V.tensor_scalar(out=ai[:], in0=bpf[:], scalar1=0.0625, scalar2=-0.46875, op0=ALU.mult, op1=ALU.add),
                     reads=[Rbpf], writes=[Rai])
                k.op('dve', lambda: V.tensor_copy(out=af[:], in_=ai[:]), reads=[Rai], writes=[Raf])
                k.op('dve', lambda: V.scalar_tensor_tensor(out=bf_[:], in0=af[:], scalar=-16.0, in1=bpf[:], op0=ALU.mult, op1=ALU.add),
                     reads=[Raf, Rbpf], writes=[Rbf])
                iot = cst[:, IOTA:IOTA + 16].unsqueeze(1).unsqueeze(1).to_broadcast([128, 8, 16, 16])
                for (src, which, dsti, Rd) in ((af, 0, i1, Ri1), (bf_, 1, i2, Ri2)):
                    k.op('dve', lambda: V.tensor_tensor(out=eqa[:], in0=src[:].unsqueeze(3).to_broadcast([128, 8, 16, 16]), in1=iot, op=ALU.is_equal),
                         reads=[Raf, Rbf, Rcst], writes=[Reqa])
                    k.op('dve', lambda: V.tensor_tensor(out=prod[:], in0=eqa[:], in1=tfv[:, :, which, :].unsqueeze(2).to_broadcast([128, 8, 16, 16]), op=ALU.mult),
                         reads=[Reqa, Rtif], writes=[Rprod])
                    k.op('dve', lambda: V.tensor_reduce(out=dsti[:], in_=prod[:], axis=AX.X, op=ALU.add), reads=[Rprod], writes=[Rd])
                k.op('dve', lambda: V.scalar_tensor_tensor(out=eidf[:], in0=i1[:].rearrange("p h k -> p (h k)"), scalar=128.0,
                                                          in1=i2[:].rearrange("p h k -> p (h k)"), op0=ALU.mult, op1=ALU.add),
                     reads=[Ri1, Ri2], writes=[Reidf])
                k.op('dve', lambda: V.tensor_copy(out=eid[:], in_=eidf[:]), reads=[Reidf], writes=[Reid])
                yield
                k.op('dve', lambda: V.tensor_tensor(out=gt[:], in0=bv[:], in1=bv[:, :, 0:1].to_broadcast([128, 8, 16]), op=ALU.subtract),
                     reads=[Rbv], writes=[Rgt])
                k.op('act', lambda: A.activation(out=gt[:], in_=gt[:], func=AF.Exp), reads=[Rgt], writes=[Rgt])
                k.op('dve', lambda: V.tensor_reduce(out=zz[:, 0:8], in_=gt[:], axis=AX.X, op=ALU.add), reads=[Rgt], writes=[Rzz])
                k.op('dve', lambda: V.reciprocal(out=zz[:, 8:16], in_=zz[:, 0:8]), reads=[Rzz], writes=[Rzz])
                k.op('dve', lambda: V.tensor_tensor(out=gt[:], in0=gt[:], in1=zz[:, 8:16].unsqueeze(2).to_broadcast([128, 8, 16]), op=ALU.mult),
                     reads=[Rgt, Rzz], writes=[Rgt])
            def run_all(g):
                for _ in g: pass
            run_all(prep(0, 0))
            for tt in range(NT):
                r0 = tt * 128
                b = tt % 2
                ht, Rht, dht = hts[b]; fb, Rfb = fbs[b]; eid, Reid = eids[b]; gt, Rgt = gts[b]
                gtf = gt[:].rearrange("p h k -> p (h k)")
                nxt = prep(tt + 1, 1 - b) if tt + 1 < NT else iter(())
                for j in range(128):
                    g_, Rg, dgb = gb[gcnt % NBUF]; gcnt += 1
                    dgt, Rdg = dg_[dcnt % 3]; dcnt += 1
                    k.op('pool', lambda: P.indirect_dma_start(out=g_[:], out_offset=None, in_=uv16[:, :],
                                                             in_offset=bass.IndirectOffsetOnAxis(ap=eid[:, j:j + 1], axis=0)),
                         reads=[Reid], writes=[Rg], dsem=dgb)
                    k.op('dve', lambda: V.scalar_tensor_tensor(out=junk[:], in0=g_[:, 0:D], scalar=1.0, in1=fb[:], op0=ALU.mult, op1=ALU.mult,
                                                              accum_out=hdn[:, j:j + 1]), reads=[Rg, Rfb], writes=[Rjunk, Rhdn[j]])
                    k.op('act', lambda: A.activation(out=wv[:, j:j + 1], in_=hdn[:, j:j + 1], func=AF.Gelu), reads=[Rhdn[j]], writes=[Rwv[j]])
                    k.op('act', lambda: A.mul(out=wv[:, j:j + 1], in_=wv[:, j:j + 1], mul=gtf[:, j:j + 1]), reads=[Rwv[j], Rgt], writes=[Rwv[j]])
                    k.op('act', lambda: A.activation(out=dgt[:], in_=identb[:], func=AF.Copy, scale=wv[:, j:j + 1]), reads=[Rwv[j], Rconst], writes=[Rdg])
                    def mmv():
                        ins = None
                        for q in range(4):
                            ins = T.matmul(pv[:, q * 512:(q + 1) * 512], dgt[:], g_[:, D + q * 512:D + (q + 1) * 512], start=(j == 0), stop=(j == 127))
                        return ins
                    k.op('pe', mmv, reads=[Rdg, Rg], writes=[Rpv])
                    if j % 4 == 3: next(nxt, None)
                run_all(nxt)
                k.op('dve', lambda: V.tensor_tensor(out=ht[:], in0=pv[:], in1=ht[:], op=ALU.add), reads=[Rpv, Rht], writes=[Rht])
                k.op('sp', lambda: SP.dma_start(out=h_s[r0:r0 + 128, :], in_=ht[:]), reads=[Rht], dsem=dst_)
            k.barrier()
        if phases < 6:
            k.op('sp', lambda: SP.dma_start(out=out_d[0:128, :], in_=xo[0:128, :]), dsem=d_c)
            SP.wait_ge(d_c.sem, d_c.val)
            return nc
        with ExitStack() as es:
            alloc_wst(es)
            wg = SB(es, "wg", [128, NCH, D], BF16); Rwg = [Res(), Res(), Res()]
            load_w16("wpg", wpg16, NCH, D, wg, Rwg)
            wp = SB(es, "wp", [128, 2, D], BF16); Rwp = [Res(), Res(), Res()]
            load_w16("wpp", wpp16, 2, D, wp, Rwp)
            gfin = SB(es, "gfin", [128, D], F32); Rgfin = Res(); dg = k.dsem("dg")
            k.op('sp', lambda: SP.dma_start(out=gfin[:], in_=gfin_d[:, :]), writes=[Rgfin], dsem=dg)
            hts = [(SB(es, f"ht5_{i}", [128, D], F32), Res(), k.dsem(f"dh{i}")) for i in range(2)]
            pts = [(SB(es, f"pt5_{i}", [128, 256], F32), Res(), k.dsem(f"dx{i}")) for i in range(2)]
            hbs = [(SB(es, f"hb{i}", [128, D], BF16), Res()) for i in range(2)]
            pbs = [(SB(es, f"pb{i}", [128, 256], BF16), Res()) for i in range(2)]
            hTs = [(SB(es, f"hT{i}", [128, NCH, 128], BF16), Res()) for i in range(2)]
            pT5s = [(SB(es, f"pT5{i}", [128, 2, 128], BF16), Res()) for i in range(2)]
            sgs = [(SB(es, f"sg{i}", [128, 512], F32), Res()) for i in range(2)]
            h3s = [(SB(es, f"h3{i}", [128, D], F32), Res()) for i in range(2)]
            ots = [(SB(es, f"ot5_{i}", [128, D], F32), Res(), k.dsem(f"dgb{i}")) for i in range(2)]
            st5s = [(SB(es, f"st5{i}", [128, 8], F32), Res()) for i in range(2)]
            psTs = [(PS(es, f"psT5{i}", [128, D], BF16), Res()) for i in range(2)]
            psP = PS(es, "psP5", [128, 256], BF16); RpsP = Res()
            psg5 = [PS(es, f"psg5_{i}", [128, 512], F32) for i in range(2)]; Rpsg5 = [Res(), Res()]
            psp5 = [PS(es, f"psp5_{i}", [128, 512], F32) for i in range(1)]; Rpsp5 = [Res()]
            pc = 0
            def front5(tt):
                r0 = tt * 128
                ht, Rht, dht = hts[tt % 2]; pt, Rpt, dpt = pts[tt % 2]; ot, Rot, dot = ots[tt % 2]
                hb, Rhb = hbs[tt % 2]; pb, Rpb = pbs[tt % 2]; hT, RhT = hTs[tt % 2]; pT5, RpT5 = pT5s[tt % 2]
                h3, Rh3 = h3s[tt % 2]; st5, Rst5 = st5s[tt % 2]; psT, RpsT = psTs[tt % 2]
                k.op('sp', lambda: SP.dma_start(out=ht[:], in_=h_s[r0:r0 + 128, :]), writes=[Rht], dsem=dht)
                k.op('sp', lambda: SP.dma_start(out=pt[:], in_=pp[r0:r0 + 128, :]), writes=[Rpt], dsem=dpt)
                k.op('act', lambda: A.activation(out=hb[:], in_=ht[:], func=AF.Square, accum_out=st5[:, 0:1]), reads=[Rht], writes=[Rhb, Rst5])
                rstd_from_ss(None, st5[:, 0:1], D, st5[:, 2:3], st5[:, 1:2], Rst5, Rst5, Rst5)
                k.op('dve', lambda: V.tensor_scalar(out=hb[:], in0=ht[:], scalar1=st5[:, 2:3], scalar2=None, op0=ALU.mult), reads=[Rht, Rst5], writes=[Rhb])
                k.op('act', lambda: A.copy(out=pb[:], in_=pt[:]), reads=[Rpt], writes=[Rpb])
                def tr():
                    ins = None
                    for c in range(NCH):
                        ins = T.transpose(psT[:, c * 128:(c + 1) * 128], hb[:, c * 128:(c + 1) * 128], identb[:])
                    return ins
                k.op('pe', tr, reads=[Rhb, Rconst], writes=[RpsT])
                def tr2():
                    ins = None
                    for c in range(2):
                        ins = T.transpose(psP[:, c * 128:(c + 1) * 128], pb[:, c * 128:(c + 1) * 128], identb[:])
                    return ins
                k.op('pe', tr2, reads=[Rpb, Rconst], writes=[RpsP])
                k.op('dve', lambda: V.tensor_tensor(out=hT[:], in0=psT[:].rearrange("p (c t) -> p c t", c=NCH),
                                                    in1=cst[:, C_GPLE:C_GPLE + NCH].unsqueeze(2).to_broadcast([128, NCH, 128]), op=ALU.mult),
                     reads=[RpsT, Rcst], writes=[RhT])
                k.op('act', lambda: A.copy(out=pT5[:].rearrange("p c t -> p (c t)"), in_=psP[:]), reads=[RpsP], writes=[RpT5])
            def back5(tt):
                nonlocal pc
                r0 = tt * 128
                ht, Rht, dht = hts[tt % 2]; pt, Rpt, dpt = pts[tt % 2]; ot, Rot, dot = ots[tt % 2]
                hb, Rhb = hbs[tt % 2]; pb, Rpb = pbs[tt % 2]; hT, RhT = hTs[tt % 2]; pT5, RpT5 = pT5s[tt % 2]
                h3, Rh3 = h3s[tt % 2]; st5, Rst5 = st5s[tt % 2]; psT, RpsT = psTs[tt % 2]
                for cq in range(4):
                    pi_ = pc % 2; pc += 1
                    sg, Rsg = sgs[pi_]
                    def mmg():
                        ins = None
                        for c in range(NCH):
                            ins = T.matmul(psg5[pi_][:], hT[:, c, :], wg[:, c, cq * 512:(cq + 1) * 512], start=(c == 0), stop=(c == NCH - 1))
                        return ins
                    def mmp():
                        ins = None
                        for c in range(2):
                            ins = T.matmul(psp5[0][:], pT5[:, c, :], wp[:, c, cq * 512:(cq + 1) * 512], start=(c == 0), stop=(c == 1))
                        return ins
                    k.op('pe', mmg, reads=[RhT] + Rwg, writes=[Rpsg5[pi_]])
                    k.op('pe', mmp, reads=[RpT5] + Rwp, writes=[Rpsp5[0]])
                    k.op('act', lambda: A.activation(out=sg[:], in_=psg5[pi_][:], func=AF.Sigmoid), reads=[Rpsg5[pi_]], writes=[Rsg])
                    k.op('dve', lambda: V.tensor_tensor(out=sg[:], in0=psp5[0][:], in1=sg[:], op=ALU.mult), reads=[Rpsp5[0], Rsg], writes=[Rsg])
                    k.op('dve', lambda: V.tensor_tensor(out=h3[:, cq * 512:(cq + 1) * 512], in0=sg[:], in1=ht[:, cq * 512:(cq + 1) * 512], op=ALU.add),
                         reads=[Rsg, Rht], writes=[Rh3])
                k.op('act', lambda: A.activation(out=hb[:], in_=h3[:], func=AF.Square, accum_out=st5[:, 4:5]), reads=[Rh3], writes=[Rhb, Rst5])
                rstd_from_ss(None, st5[:, 4:5], D, st5[:, 6:7], st5[:, 5:6], Rst5, Rst5, Rst5)
                k.op('dve', lambda: V.scalar_tensor_tensor(out=ot[:], in0=h3[:], scalar=st5[:, 6:7], in1=gfin[:], op0=ALU.mult, op1=ALU.mult),
                     reads=[Rh3, Rst5, Rgfin], writes=[Rot])
                k.op('sp', lambda: SP.dma_start(out=out_d[r0:r0 + 128, :], in_=ot[:]), reads=[Rot], dsem=dot)
            front5(0)
            for tt in range(NT):
                if tt + 1 < NT: front5(tt + 1)
                back5(tt)
            k.barrier()
    return nc


def prep(inp, NBH):
    TP = NBH * 512
    f32 = np.float32
    x = np.asarray(inp['x'], f32); p = np.asarray(inp['p'], f32); pos = np.asarray(inp['positions'], np.int32)
    def fm(g, n): return np.ascontiguousarray(np.asarray(g, f32).reshape(n, 128).T)
    cst = np.zeros((128, 128), f32)
    invf = (np.float32(10000.0) ** (-(np.arange(0, 64, 2, dtype=f32) / np.float32(64)))).astype(f32)
    cst[0:64, 0] = np.concatenate([invf, invf]); cst[0:32, 1] = -1.0; cst[32:64, 1] = 1.0
    cst[:, 2:18] = fm(inp['attn_norm'][0], 16); cst[:, 18:22] = fm(inp['q_norm'][0], 4); cst[:, 22:24] = fm(inp['kv_norm'][0], 2)
    cst[:, 24:32] = fm(inp['conv_out_norm'][0], 8); cst[:, 32:40] = fm(inp['attn_out_norm'][0], 8)
    cst[:, 40:56] = fm(inp['ffn_norm'][0], 16); cst[:, 56:72] = fm(inp['ple_norm'][0], 16)
    cw = np.asarray(inp['conv_w'][0], f32)
    cst[:, 72:96] = cw.reshape(3, 8, 128).transpose(2, 1, 0).reshape(128, 24)
    cst[:, 96:112] = np.arange(16, dtype=f32)[None]
    w_in = np.ascontiguousarray(np.asarray(inp['w_in'][0], f32))
    w_in_sw = np.ascontiguousarray(np.concatenate([w_in[:, 3872:3904], w_in[:, 3840:3872]], axis=1))
    w_uq = np.ascontiguousarray(np.asarray(inp['w_uq'][0], f32))
    wr = w_uq.reshape(512, 8, 192)
    w_uq_sw = np.ascontiguousarray(np.concatenate([wr[:, :, 160:192], wr[:, :, 128:160]], axis=2).reshape(512, 512))
    sk = np.asarray(inp['sub_keys'][0], f32)
    skT = np.ascontiguousarray(sk.transpose(3, 0, 1, 2).reshape(128, 2048))
    shared = dict(cst=cst, ident=np.eye(128, dtype=f32), w_in=w_in, w_in_sw=w_in_sw, w_uq=w_uq, w_uq_sw=w_uq_sw,
                  w_ukv=np.ascontiguousarray(np.asarray(inp['w_ukv'][0], f32)), w_out=np.ascontiguousarray(np.asarray(inp['w_out'][0], f32)),
                  w_pq=np.ascontiguousarray(np.asarray(inp['w_pq'][0], f32)), w_pg=np.ascontiguousarray(np.asarray(inp['w_ple_gate'][0], f32)),
                  w_pp=np.ascontiguousarray(np.asarray(inp['w_ple_proj'][0], f32)), skT=skT,
                  gffn_rep=np.ascontiguousarray(np.tile(np.asarray(inp['ffn_norm'][0], f32)[None], (128, 1))),
                  gfin_rep=np.ascontiguousarray(np.tile(np.asarray(inp['final_norm'], f32)[None], (128, 1))),
                  u_tab=np.ascontiguousarray(np.asarray(inp['u_tab'][0], f32)), v_tab=np.ascontiguousarray(np.asarray(inp['v_tab'][0], f32)))
    maps = []
    for c in range(N_CORES):
        b, half = c // 2, c % 2
        own = slice(half * TP, (half + 1) * TP)
        m = dict(shared)
        m['xo'] = np.ascontiguousarray(x[b, own])
        m['pp'] = np.ascontiguousarray(p[0, b, own])
        kb = np.zeros((1, 2 * TP), f32)
        if half == 1:
            m['xp'] = np.ascontiguousarray(x[b, 0:TP]); ppos = pos[b, 0:TP]
        else:
            m['xp'] = np.zeros((TP, D), f32); ppos = np.zeros((TP,), np.int32); kb[0, 0:TP] = -30000.0
        m['posr'] = np.ascontiguousarray(np.concatenate([ppos, pos[b, own]])[None].astype(np.int32))
        m['kbias'] = kb
        maps.append(m)
    return maps


_NC_CACHE = {}


def kernel(**inputs):
    S = np.asarray(inputs['x']).shape[1]
    NBH = S // 1024
    if NBH not in _NC_CACHE:
        _NC_CACHE[NBH] = build(NBH)
    nc = _NC_CACHE[NBH]
    maps = prep(inputs, NBH)
    res = run_bass_kernel_spmd(nc, maps, core_ids=list(range(N_CORES)))
    B = np.asarray(inputs['x']).shape[0]
    out = np.zeros((B, S, D), np.float32)
    TP = NBH * 512
    for c in range(N_CORES):
        b, half = c // 2, c % 2
        out[b, half * TP:(half + 1) * TP] = res.results[c]['out']
    return out
```

```python
import numpy as np
from contextlib import ExitStack
import concourse.bass as bass
import concourse.mybir as mybir
from concourse.bass_utils import run_bass_kernel_spmd

F32 = mybir.dt.float32; BF16 = mybir.dt.bfloat16; I32 = mybir.dt.int32; U32 = mybir.dt.uint32
AF = mybir.ActivationFunctionType; ALU = mybir.AluOpType; AX = mybir.AxisListType

D = 2048; NCH = 16
EPS = 1e-6
SCALE = 192.0 ** -0.5
N_CORES = 8


class Ev:
    __slots__ = ('sem', 'val', 'eng', 'dma')
    def __init__(s, sem, val, eng, dma): s.sem = sem; s.val = val; s.eng = eng; s.dma = dma


class Res:
    def __init__(s, name='r'): s.name = name; s.w = None; s.r = {}


class DSem:
    def __init__(s, nc, name): s.sem = nc.alloc_semaphore(name); s.val = 0


class KB:
    def __init__(self, nc):
        self.nc = nc
        self.E = {'pe': nc.tensor, 'act': nc.scalar, 'dve': nc.vector, 'pool': nc.gpsimd, 'sp': nc.sync}
        self.sem = {e: nc.alloc_semaphore('s_' + e) for e in ('pe', 'act', 'dve', 'pool')}
        self.cnt = {e: 0 for e in self.sem}
        self.waited = {e: {} for e in self.E}
        self.dsems = []
        self.dsem_by = {}
    def dsem(self, name):
        if name in self.dsem_by: return self.dsem_by[name]
        d = DSem(self.nc, name); self.dsems.append(d); self.dsem_by[name] = d; return d
    def _wait(self, eng, ev):
        kk = ev.sem.num
        if self.waited[eng].get(kk, 0) < ev.val:
            self.E[eng].wait_ge(ev.sem, ev.val)
            self.waited[eng][kk] = ev.val
    def op(self, eng, fn, reads=(), writes=(), dsem=None):
        deps = []
        for r in reads:
            if r.w is not None: deps.append(r.w)
        for w in writes:
            if w.w is not None and (w.w.dma or w.w.eng != eng or eng != 'pe'): deps.append(w.w)
            for ev in w.r.values():
                deps.append(ev)
        for ev in deps: self._wait(eng, ev)
        ins = fn()
        if dsem is not None:
            dsem.val += 16; ins.then_inc(dsem.sem, 16); ev = Ev(dsem.sem, dsem.val, eng, True)
        else:
            self.cnt[eng] += 1; ins.then_inc(self.sem[eng], 1); ev = Ev(self.sem[eng], self.cnt[eng], eng, False)
        for r in reads:
            r.r[(ev.sem.num)] = ev
        for w in writes:
            w.w = ev; w.r = {}
        return ev
    def barrier(self):
        for e in self.E:
            for o in self.sem:
                if o != e and self.cnt[o] > 0:
                    self._wait(e, Ev(self.sem[o], self.cnt[o], o, False))
            for d in self.dsems:
                if d.val > 0: self._wait(e, Ev(d.sem, d.val, 'x', True))


def build(NBH, phases=6, dbg=False):
    TP = NBH * 512; NK = 2 * TP; NQ = TP; NKT = NK // 128; NKTP = TP // 128; NQB = NQ // 512
    nc = bass.Bass("TRN2", target_bir_lowering=False)
    k = KB(nc)
    def din(name, shape, dt=F32): return nc.dram_tensor(name, shape, dt, kind="ExternalInput").ap()
    skind = "ExternalOutput" if dbg else "Internal"
    def dscr(name, shape, dt): return nc.dram_tensor(name, shape, dt, kind=skind).ap()
    xp = din("xp", [TP, D]); xo = din("xo", [TP, D]); pp = din("pp", [TP, 256])
    posr = din("posr", [1, NK], I32); kbias = din("kbias", [1, NK])
    cst_d = din("cst", [128, 128]); ident_d = din("ident", [128, 128])
    w_in = din("w_in", [D, 3904]); w_in_sw = din("w_in_sw", [D, 64])
    w_uq = din("w_uq", [512, 1536]); w_uq_sw = din("w_uq_sw", [512, 512]); w_ukv = din("w_ukv", [256, 2048])
    w_out = din("w_out", [D, D]); w_pq = din("w_pq", [D, D]); w_pg = din("w_pg", [D, D]); w_pp = din("w_pp", [256, D])
    skT_d = din("skT", [128, 2048]); gffn_d = din("gffn_rep", [128, D]); gfin_d = din("gfin_rep", [128, D])
    if phases >= 5:
        u_tab = din("u_tab", [16384, D]); v_tab = din("v_tab", [16384, D])
    out_d = nc.dram_tensor("out", [TP, D], F32, kind="ExternalOutput").ap()
    ckvn_s = dscr("ckvn_s", [2, 128, NK], BF16); kr_s = dscr("kr_s", [64, NK], BF16)
    cqn_s = dscr("cqn_s", [4, 128, NQ], BF16); cs_s = dscr("cs_s", [64, NQ], F32); sn_s = dscr("sn_s", [64, NQ], F32)
    mix_s = dscr("mix_s", [16, 128, NQ], BF16); h_s = dscr("h_s", [TP, D], F32)
    uv16 = nc.dram_tensor("uv16", [16384, 2 * D], BF16, kind="Internal").ap()
    def d16(name, shape): return nc.dram_tensor(name, shape, BF16, kind="Internal").ap()
    wc16 = d16("wc16", [D, 3072]); wq16 = d16("wq16", [512, 2048]); wkv16 = d16("wkv16", [256, 2048])
    wo16 = d16("wo16", [D, D]); wpq16 = d16("wpq16", [D, D]); wpg16 = d16("wpg16", [D, D]); wpp16 = d16("wpp16", [256, D]); sk16 = d16("sk16", [128, D])

    C_INVF, C_SSC, C_GATTN, C_GQ, C_GKV, C_GCO, C_GAO, C_GFFN, C_GPLE, C_CW = 0, 1, 2, 18, 22, 24, 32, 40, 56, 72

    es_all = ExitStack()
    uid = [0]
    def SB(es, name, shape, dt):
        uid[0] += 1
        return es.enter_context(nc.sbuf_tensor(f"sb{uid[0]}_{name}", shape, dt))
    def PS(es, name, shape, dt):
        uid[0] += 1
        return es.enter_context(nc.psum_tensor(f"ps{uid[0]}_{name}", shape, dt))
    V = nc.vector; A = nc.scalar; P = nc.gpsimd; T = nc.tensor; SP = nc.sync

    with nc.allow_low_precision("bf16 matmul operands, fp32 accumulation"), es_all:
        cst = SB(es_all, "cst", [128, 128], F32); Rcst = Res()
        identb = SB(es_all, "identb", [128, 128], BF16); ones32 = SB(es_all, "ones32", [128, 128], F32)
        onesb = SB(es_all, "onesb", [128, 128], BF16); epsc = SB(es_all, "epsc", [128, 1], F32); tri = SB(es_all, "tri", [128, 128], BF16)
        Rconst = Res()
        wst = []
        def alloc_wst(es_):
            wst.clear()
            for i in range(3):
                wst.append((SB(es_, f"wst{i}", [128, 1024], F32), Res(), k.dsem(f"dw{i}")))
        es_w0 = ExitStack(); alloc_wst(es_w0)
        d_c = k.dsem("d_c")
        k.op('sp', lambda: SP.dma_start(out=cst[:], in_=cst_d[:, :]), writes=[Rcst], dsem=d_c)
        k.op('sp', lambda: SP.dma_start(out=wst[0][0][:, 0:128], in_=ident_d[:, :]), writes=[wst[0][1]], dsem=wst[0][2])
        k.op('dve', lambda: V.tensor_copy(out=identb[:], in_=wst[0][0][:, 0:128]), reads=[wst[0][1]], writes=[Rconst])
        k.op('dve', lambda: V.memset(ones32[:], 1.0), writes=[Rconst])
        k.op('dve', lambda: V.memset(onesb[:], 1.0), writes=[Rconst])
        k.op('dve', lambda: V.memset(epsc[:], EPS), writes=[Rconst])
        k.op('pool', lambda: P.memset(tri[:], 1.0), writes=[Rconst])
        k.op('pool', lambda: P.affine_select(out=tri[:], in_=tri[:], pattern=[[1, 128]], compare_op=ALU.is_ge, fill=0.0,
                                             base=0, channel_multiplier=-1), reads=[Rconst], writes=[Rconst])
        wcnt = [0]
        def load_weight(src, nchunks, col0, ncols, dst, dres, dcol0):
            for c in range(nchunks):
                for cc in range(0, ncols, 1024):
                    w = min(1024, ncols - cc)
                    slot = wcnt[0] % 3; wcnt[0] += 1
                    st, rst, ds = wst[slot]
                    k.op('sp', lambda: SP.dma_start(out=st[:, 0:w], in_=src[c * 128:(c + 1) * 128, col0 + cc:col0 + cc + w]),
                         writes=[rst], dsem=ds)
                    o = dst[:, c, dcol0 + cc:dcol0 + cc + w]
                    if slot == 0: k.op('act', lambda: A.copy(out=o, in_=st[:, 0:w]), reads=[rst], writes=[dres[0]])
                    elif slot == 1: k.op('pool', lambda: P.tensor_copy(out=o, in_=st[:, 0:w]), reads=[rst], writes=[dres[1]])
                    else: k.op('dve', lambda: V.tensor_copy(out=o, in_=st[:, 0:w]), reads=[rst], writes=[dres[2]])

        def rstd_from_ss(es_unused, ss_ap, n, out_ap, tmp_ap, Rin, Rtmp, Rout):
            k.op('act', lambda: A.activation(out=tmp_ap, in_=ss_ap, func=AF.Sqrt, scale=1.0 / n, bias=epsc[0:tmp_ap.shape[0], 0:1]),
                 reads=[Rin, Rconst], writes=[Rtmp])
            k.op('dve', lambda: V.reciprocal(out=out_ap, in_=tmp_ap), reads=[Rtmp], writes=[Rout])

        class XA:
            pass
        def make_xa(es):
            xa = XA()
            xa.xst = [(SB(es, f"xst{i}", [128, D], F32), Res(), k.dsem(f"dx{i}")) for i in range(2)]
            xa.xs = SB(es, "xs", [128, D], BF16); xa.Rxs = Res()
            xa.st = SB(es, "xstat", [128, 4], F32); xa.Rst = Res()
            xa.psT = PS(es, "psT", [128, D], BF16); xa.RpsT = Res()
            xa.cnt = 0
            return xa
        def make_aT(xa, src, row0, ntiles, gcol, aT, RaT):
            for j in range(ntiles):
                xt, Rx, dx = xa.xst[xa.cnt % 2]; xa.cnt += 1
                k.op('sp', lambda: SP.dma_start(out=xt[:], in_=src[row0 + j * 128: row0 + (j + 1) * 128, :]), writes=[Rx], dsem=dx)
                k.op('act', lambda: A.activation(out=xa.xs[:], in_=xt[:], func=AF.Square, accum_out=xa.st[:, 0:1]),
                     reads=[Rx], writes=[xa.Rxs, xa.Rst])
                rstd_from_ss(None, xa.st[:, 0:1], D, xa.st[:, 2:3], xa.st[:, 1:2], xa.Rst, xa.Rst, xa.Rst)
                k.op('dve', lambda: V.tensor_scalar(out=xa.xs[:], in0=xt[:], scalar1=xa.st[:, 2:3], scalar2=None, op0=ALU.mult),
                     reads=[Rx, xa.Rst], writes=[xa.Rxs])
                def tr():
                    ins = None
                    for c in range(NCH):
                        ins = T.transpose(xa.psT[:, c * 128:(c + 1) * 128], xa.xs[:, c * 128:(c + 1) * 128], identb[:])
                    return ins
                k.op('pe', tr, reads=[xa.Rxs, Rconst], writes=[xa.RpsT])
                k.op('dve', lambda: V.tensor_tensor(out=aT[:, :, j * 128:(j + 1) * 128],
                                                    in0=xa.psT[:].rearrange("p (c t) -> p c t", c=NCH),
                                                    in1=cst[:, gcol:gcol + NCH].unsqueeze(2).to_broadcast([128, NCH, 128]),
                                                    op=ALU.mult), reads=[xa.RpsT, Rcst], writes=[RaT])

        k.barrier(); es_w0.close()
        R16 = {}
        cast_jobs = []
        def cast_w(name, dst, src, nrows, col0, ncols, dcol0=0):
            ds = k.dsem("dpw_" + name); r = R16.setdefault(name, Res())
            for r0_ in range(0, nrows, 256):
                n = min(256, nrows - r0_)
                cast_jobs.append((lambda dst=dst, src=src, r0_=r0_, n=n, dcol0=dcol0, ncols=ncols, col0=col0:
                                  P.dma_start(out=dst[r0_:r0_ + n, dcol0:dcol0 + ncols], in_=src[r0_:r0_ + n, col0:col0 + ncols]), r, ds))
        cast_w("wc", wc16, w_in, D, 0, 3072)
        cast_w("wq", wq16, w_uq, 512, 0, 1536); cast_w("wq", wq16, w_uq_sw, 512, 0, 512, 1536)
        cast_w("wkv", wkv16, w_ukv, 256, 0, 2048)
        cast_w("wo", wo16, w_out, D, 0, D)
        if phases >= 5:
            cast_w("wpq", wpq16, w_pq, D, 0, D); cast_w("sk", sk16, skT_d, 128, 0, D)
        if phases >= 6:
            cast_w("wpg", wpg16, w_pg, D, 0, D); cast_w("wpp", wpp16, w_pp, 256, 0, D)
        def emit_casts(nmax):
            for _ in range(nmax):
                if not cast_jobs: return
                fn, r, ds = cast_jobs.pop(0)
                k.op('pool', fn, writes=[r], dsem=ds)
        def load_w16(name, src16, nchunks, ncols, dst, dres):
            ds = k.dsem("dlw_" + name)
            for c in range(nchunks):
                k.op('sp', lambda: SP.dma_start(out=dst[:, c, 0:ncols], in_=src16[c * 128:(c + 1) * 128, 0:ncols]), reads=[R16[name]], writes=[dres[0]], dsem=ds)
        with ExitStack() as es:
            alloc_wst(es)
            wl = SB(es, "wl", [128, NCH, 896], BF16); Rwl = [Res(), Res(), Res()]
            load_weight(w_in, NCH, 3072, 832, wl, Rwl, 0)
            load_weight(w_in_sw, NCH, 0, 64, wl, Rwl, 832)
            xa = make_xa(es)
            aT = SB(es, "aT", [128, NCH, 512], BF16); RaT = Res()
            posi = SB(es, "posi", [64, 512], I32); Rposi = Res(); dpos = k.dsem("dpos")
            ang = SB(es, "ang", [64, 512], F32); Rang = Res()
            r1 = SB(es, "r1", [64, 512], F32); Rr1 = Res()
            ki = SB(es, "ki", [64, 512], I32); Rki = Res()
            kf = SB(es, "kf", [64, 512], F32); Rkf = Res()
            cs = SB(es, "cs", [64, 512], F32); Rcs = Res()
            sn = SB(es, "sn", [64, 512], F32); Rsn = Res()
            lat = [SB(es, f"lat{i}", [128, 512], F32) for i in range(4)]; Rlat = [Res() for _ in range(4)]
            sq = SB(es, "sq", [128, 512], F32); Rsq = Res()
            sd = SB(es, "sd", [128, 512], F32); Rsd = Res()
            rs = SB(es, "rs", [128, 512], F32); Rrs = Res()
            latb = [SB(es, f"latb{i}", [128, 512], BF16) for i in range(2)]; Rlatb = [Res(), Res()]; dlat = [k.dsem("dlat0"), k.dsem("dlat1")]
            t1 = SB(es, "t1", [64, 512], F32); Rt1 = Res()
            t2 = SB(es, "t2", [64, 512], F32); Rt2 = Res()
            krb = SB(es, "krb", [64, 512], BF16); Rkrb = Res(); dkr = k.dsem("dkr")
            dcs = k.dsem("dcs")
            psl = [PS(es, f"psl{i}", [128, 512], F32) for i in range(4)]; Rpsl = [Res() for _ in range(4)]
            pss = PS(es, "pss", [128, 512], F32); Rpss = Res()
            lcnt = [0]

            def normed_group(nchunk, wcol0, gcol, nfeat, dst_s, col_off):
                for j in range(nchunk):
                    def mm(j=j):
                        ins = None
                        for c in range(NCH):
                            ins = T.matmul(psl[j][:], wl[:, c, wcol0 + j * 128: wcol0 + (j + 1) * 128], aT[:, c, :],
                                           start=(c == 0), stop=(c == NCH - 1))
                        return ins
                    k.op('pe', mm, reads=[RaT] + Rwl, writes=[Rpsl[j]])
                    k.op('act', lambda j=j: A.copy(out=lat[j][:], in_=psl[j][:]), reads=[Rpsl[j]], writes=[Rlat[j]])
                    k.op('act', lambda j=j: A.activation(out=sq[:], in_=lat[j][:], func=AF.Square), reads=[Rlat[j]], writes=[Rsq])
                    k.op('pe', lambda j=j: T.matmul(pss[:], ones32[:], sq[:], start=(j == 0), stop=(j == nchunk - 1)),
                         reads=[Rsq, Rconst], writes=[Rpss])
                rstd_from_ss(None, pss[:], nfeat, rs[:], sd[:], Rpss, Rsd, Rrs)
                for j in range(nchunk):
                    lb, Rlb, dl = latb[lcnt[0] % 2], Rlatb[lcnt[0] % 2], dlat[lcnt[0] % 2]; lcnt[0] += 1
                    k.op('dve', lambda j=j: V.scalar_tensor_tensor(out=lb[:], in0=lat[j][:], scalar=cst[:, gcol + j:gcol + j + 1], in1=rs[:],
                                                                  op0=ALU.mult, op1=ALU.mult), reads=[Rlat[j], Rrs, Rcst], writes=[Rlb])
                    k.op('sp', lambda j=j: SP.dma_start(out=dst_s[j, :, col_off:col_off + 512], in_=lb[:]), reads=[Rlb], dsem=dl)

            for blk in range(2 * NBH):
                own = blk >= NBH
                src = xo if own else xp
                make_aT(xa, src, (blk % NBH) * 512, 4, C_GATTN, aT, RaT)
                k.op('sp', lambda: SP.dma_start(out=posi[:], in_=posr[:, blk * 512:(blk + 1) * 512].partition_broadcast(64)),
                     writes=[Rposi], dsem=dpos)
                k.op('dve', lambda: V.tensor_copy(out=ang[:], in_=posi[:]), reads=[Rposi], writes=[Rang])
                k.op('dve', lambda: V.tensor_scalar(out=ang[:], in0=ang[:], scalar1=cst[0:64, C_INVF:C_INVF + 1], scalar2=None, op0=ALU.mult),
                     reads=[Rang, Rcst], writes=[Rang])
                for which in range(2):
                    if which == 0:
                        k.op('dve', lambda: V.tensor_scalar(out=ki[:], in0=ang[:], scalar1=float(1 / (2 * np.pi)), scalar2=None, op0=ALU.mult),
                             reads=[Rang], writes=[Rki])
                        src_ang = ang
                    else:
                        k.op('dve', lambda: V.tensor_scalar(out=r1[:], in0=ang[:], scalar1=float(np.pi / 2), scalar2=None, op0=ALU.add),
                             reads=[Rang], writes=[Rr1])
                        k.op('dve', lambda: V.tensor_scalar(out=ki[:], in0=r1[:], scalar1=float(1 / (2 * np.pi)), scalar2=None, op0=ALU.mult),
                             reads=[Rr1], writes=[Rki])
                        src_ang = r1
                    k.op('dve', lambda: V.tensor_copy(out=kf[:], in_=ki[:]), reads=[Rki], writes=[Rkf])
                    k.op('dve', lambda: V.scalar_tensor_tensor(out=r1[:], in0=kf[:], scalar=-6.28125, in1=src_ang[:], op0=ALU.mult, op1=ALU.add),
                         reads=[Rkf, Rang, Rr1], writes=[Rr1])
                    k.op('dve', lambda: V.scalar_tensor_tensor(out=r1[:], in0=kf[:], scalar=-(2 * np.pi - 6.28125), in1=r1[:], op0=ALU.mult, op1=ALU.add),
                         reads=[Rkf, Rr1], writes=[Rr1])
                    k.op('dve', lambda: V.tensor_scalar(out=r1[:], in0=r1[:], scalar1=3.1415925, scalar2=-3.1415925, op0=ALU.min, op1=ALU.max),
                         reads=[Rr1], writes=[Rr1])
                    if which == 0:
                        k.op('act', lambda: A.activation(out=sn[:], in_=r1[:], func=AF.Sin, scale=cst[0:64, C_SSC:C_SSC + 1]),
                             reads=[Rr1, Rcst], writes=[Rsn])
                    else:
                        k.op('act', lambda: A.activation(out=cs[:], in_=r1[:], func=AF.Sin), reads=[Rr1], writes=[Rcs])
                if own:
                    qoff = (blk - NBH) * 512
                    k.op('sp', lambda: SP.dma_start(out=cs_s[:, qoff:qoff + 512], in_=cs[:]), reads=[Rcs], dsem=dcs)
                    k.op('sp', lambda: SP.dma_start(out=sn_s[:, qoff:qoff + 512], in_=sn[:]), reads=[Rsn], dsem=dcs)
                normed_group(2, 512, C_GKV, 256, ckvn_s, blk * 512)
                def mmkr(col, j):
                    ins = None
                    for c in range(NCH):
                        ins = T.matmul(psl[j][0:64, :], wl[:, c, col:col + 64], aT[:, c, :], start=(c == 0), stop=(c == NCH - 1))
                    return ins
                k.op('pe', lambda: mmkr(768, 2), reads=[RaT] + Rwl, writes=[Rpsl[2]])
                k.op('pe', lambda: mmkr(832, 3), reads=[RaT] + Rwl, writes=[Rpsl[3]])
                k.op('dve', lambda: V.tensor_tensor(out=t1[:], in0=psl[2][0:64, :], in1=cs[:], op=ALU.mult), reads=[Rpsl[2], Rcs], writes=[Rt1])
                k.op('dve', lambda: V.tensor_tensor(out=t2[:], in0=psl[3][0:64, :], in1=sn[:], op=ALU.mult), reads=[Rpsl[3], Rsn], writes=[Rt2])
                k.op('dve', lambda: V.tensor_tensor(out=krb[:], in0=t1[:], in1=t2[:], op=ALU.add), reads=[Rt1, Rt2], writes=[Rkrb])
                k.op('sp', lambda: SP.dma_start(out=kr_s[:, blk * 512:(blk + 1) * 512], in_=krb[:]), reads=[Rkrb], dsem=dkr)
                if own:
                    normed_group(4, 0, C_GQ, 512, cqn_s, (blk - NBH) * 512)
                emit_casts(6 if blk > 0 else 0)
            emit_casts(1000)
            k.barrier()
        if phases < 2:
            k.op('sp', lambda: SP.dma_start(out=out_d[0:128, :], in_=xo[0:128, :]), dsem=d_c)
            SP.wait_ge(d_c.sem, d_c.val)
            return nc
        with ExitStack() as es:
            alloc_wst(es)
            wc = SB(es, "wc", [128, NCH, 3072], BF16); Rwc = [Res(), Res(), Res()]
            load_w16("wc", wc16, NCH, 3072, wc, Rwc)
            xa = make_xa(es)
            aT = SB(es, "aT", [128, NCH, 512], BF16); RaT = Res()
            carry = SB(es, "carry", [128, 8, 2], F32); Rcarry = Res()
            xsb = SB(es, "xsb", [128, 512], F32); Rxsb = Res()
            ub = SB(es, "ub", [128, 514], F32); Rub = Res()
            yb = SB(es, "yb", [128, 512], F32); Ryb = Res()
            ob = SB(es, "ob", [128, 512], F32); Rob = Res()
            sq = SB(es, "sq", [128, 512], F32); Rsq = Res()
            sd = SB(es, "sd", [128, 512], F32); Rsd = Res()
            rs = SB(es, "rs", [128, 512], F32); Rrs = Res()
            onb = [SB(es, f"onb{i}", [128, 512], BF16) for i in range(2)]; Ronb = [Res(), Res()]; don = [k.dsem("don0"), k.dsem("don1")]
            psx = PS(es, "psx", [128, 512], F32); Rpsx = Res()
            psb = PS(es, "psb", [128, 512], F32); Rpsb = Res()
            psc = PS(es, "psc", [128, 512], F32); Rpsc = Res()
            pss = PS(es, "pss", [128, 512], F32); Rpss = Res()
            def mmc(ps, col, n):
                ins = None
                for c in range(NCH):
                    ins = T.matmul(ps[:, 0:n], wc[:, c, col:col + 128], aT[:, c, 0:n], start=(c == 0), stop=(c == NCH - 1))
                return ins
            make_aT(xa, xp, TP - 128, 1, C_GATTN, aT, RaT)
            for g in range(8):
                k.op('pe', lambda: mmc(psx, g * 128, 128), reads=[RaT] + Rwc, writes=[Rpsx])
                k.op('pe', lambda: mmc(psc, 2048 + g * 128, 128), reads=[RaT] + Rwc, writes=[Rpsc])
                k.op('act', lambda: A.copy(out=xsb[:, 0:128], in_=psx[:, 0:128]), reads=[Rpsx], writes=[Rxsb])
                k.op('dve', lambda: V.tensor_tensor(out=ub[:, 0:128], in0=psc[:, 0:128], in1=xsb[:, 0:128], op=ALU.mult),
                     reads=[Rpsc, Rxsb], writes=[Rub])
                k.op('dve', lambda: V.tensor_copy(out=carry[:, g, :], in_=ub[:, 126:128]), reads=[Rub], writes=[Rcarry])
            ocnt = 0
            for blk in range(NBH):
                make_aT(xa, xo, blk * 512, 4, C_GATTN, aT, RaT)
                for g in range(8):
                    k.op('pe', lambda: mmc(psx, g * 128, 512), reads=[RaT] + Rwc, writes=[Rpsx])
                    k.op('pe', lambda: mmc(psc, 2048 + g * 128, 512), reads=[RaT] + Rwc, writes=[Rpsc])
                    k.op('pe', lambda: mmc(psb, 1024 + g * 128, 512), reads=[RaT] + Rwc, writes=[Rpsb])
                    k.op('act', lambda: A.copy(out=xsb[:], in_=psx[:]), reads=[Rpsx], writes=[Rxsb])
                    k.op('dve', lambda: V.tensor_copy(out=ub[:, 0:2], in_=carry[:, g, :]), reads=[Rcarry], writes=[Rub])
                    k.op('dve', lambda: V.tensor_tensor(out=ub[:, 2:514], in0=psc[:], in1=xsb[:], op=ALU.mult), reads=[Rpsc, Rxsb], writes=[Rub])
                    cw = C_CW + g * 3
                    k.op('dve', lambda: V.tensor_scalar(out=yb[:], in0=ub[:, 2:514], scalar1=cst[:, cw + 2:cw + 3], scalar2=None, op0=ALU.mult),
                         reads=[Rub, Rcst], writes=[Ryb])
                    k.op('dve', lambda: V.scalar_tensor_tensor(out=yb[:], in0=ub[:, 1:513], scalar=cst[:, cw + 1:cw + 2], in1=yb[:], op0=ALU.mult, op1=ALU.add),
                         reads=[Rub, Rcst, Ryb], writes=[Ryb])
                    k.op('dve', lambda: V.scalar_tensor_tensor(out=yb[:], in0=ub[:, 0:512], scalar=cst[:, cw:cw + 1], in1=yb[:], op0=ALU.mult, op1=ALU.add),
                         reads=[Rub, Rcst, Ryb], writes=[Ryb])
                    k.op('dve', lambda: V.tensor_copy(out=carry[:, g, :], in_=ub[:, 512:514]), reads=[Rub], writes=[Rcarry])
                    k.op('dve', lambda: V.tensor_tensor(out=ob[:], in0=psb[:], in1=yb[:], op=ALU.mult), reads=[Rpsb, Ryb], writes=[Rob])
                    k.op('act', lambda: A.activation(out=sq[:], in_=ob[:], func=AF.Square), reads=[Rob], writes=[Rsq])
                    k.op('pe', lambda: T.matmul(pss[:], ones32[:], sq[:], start=True, stop=True), reads=[Rsq, Rconst], writes=[Rpss])
                    rstd_from_ss(None, pss[:], 128, rs[:], sd[:], Rpss, Rsd, Rrs)
                    on, Ron, dn = onb[ocnt % 2], Ronb[ocnt % 2], don[ocnt % 2]; ocnt += 1
                    k.op('dve', lambda: V.scalar_tensor_tensor(out=on[:], in0=ob[:], scalar=cst[:, C_GCO + g:C_GCO + g + 1], in1=rs[:],
                                                              op0=ALU.mult, op1=ALU.mult), reads=[Rob, Rrs, Rcst], writes=[Ron])
                    k.op('sp', lambda: SP.dma_start(out=mix_s[g, :, blk * 512:(blk + 1) * 512], in_=on[:]), reads=[Ron], dsem=dn)
            k.barrier()
        if phases < 3:
            k.op('sp', lambda: SP.dma_start(out=out_d[0:128, :], in_=xo[0:128, :]), dsem=d_c)
            SP.wait_ge(d_c.sem, d_c.val)
            return nc
        with ExitStack() as es:
            wq = SB(es, "wq", [128, 4, 2048], BF16); Rwq = [Res(), Res(), Res()]
            wkv = SB(es, "wkv", [128, 2, 2048], BF16); Rwkv = [Res(), Res(), Res()]
            ckv = SB(es, "ckv", [128, 2, NK], BF16); Rckv = Res(); dl2 = k.dsem("dl2")
            krT = SB(es, "krT", [65, NK], BF16); RkrT = Res()
            cqn = SB(es, "cqn", [128, 4, NQ], BF16); Rcqn = Res()
            es_w2 = ExitStack(); alloc_wst(es_w2)
            load_w16("wq", wq16, 4, 2048, wq, Rwq)
            load_w16("wkv", wkv16, 2, 2048, wkv, Rwkv)
            for j in range(2):
                k.op('sp', lambda: SP.dma_start(out=ckv[:, j, :], in_=ckvn_s[j, :, :]), writes=[Rckv], dsem=dl2)
            for j in range(4):
                k.op('sp', lambda: SP.dma_start(out=cqn[:, j, :], in_=cqn_s[j, :, :]), writes=[Rcqn], dsem=dl2)
            k.op('sp', lambda: SP.dma_start(out=krT[0:64, :], in_=kr_s[:, :]), writes=[RkrT], dsem=dl2)
            for cc in range(0, NK, 1024):
                st, rst, ds = wst[0]
                k.op('sp', lambda: SP.dma_start(out=st[64:65, 0:1024], in_=kbias[:, cc:cc + 1024]), writes=[rst], dsem=ds)
                k.op('dve', lambda: V.tensor_copy(out=krT[64:65, cc:cc + 1024], in_=st[64:65, 0:1024]), reads=[rst], writes=[RkrT])
            k.barrier(); es_w2.close()
            Khs = [(SB(es, f"Kh{i}", [128, NK], BF16), Res()) for i in range(2)]
            Vhs = [(SB(es, f"Vh{i}", [128, NKT, 128], BF16), Res()) for i in range(2)]
            qns = [(SB(es, f"qn{i}", [128, 512], BF16), Res()) for i in range(2)]
            qrs = [(SB(es, f"qr{i}", [65, 512], BF16), Res()) for i in range(2)]
            for i in range(2):
                k.op('dve', lambda: V.memset(qrs[i][0][64:65, :], 1.0), writes=[qrs[i][1]])
            csqs = [(SB(es, f"csq{i}", [64, 512], F32), SB(es, f"snq{i}", [64, 512], F32), Res(), k.dsem(f"dcsq{i}")) for i in range(2)]
            t1 = SB(es, "t1", [64, 512], F32); Rt1 = Res()
            t2 = SB(es, "t2", [64, 512], F32); Rt2 = Res()
            pT = [SB(es, f"pT{i}", [128, 512], BF16) for i in range(3)]; RpT = [Res() for _ in range(3)]
            rden = SB(es, "rden", [128, 512], F32); Rrden = Res()
            daccs = [(SB(es, f"dacc{i}", [128, 512], F32), Res()) for i in range(2)]
            ob = SB(es, "ob", [128, 512], F32); Rob = Res()
            sq = SB(es, "sq", [128, 512], F32); Rsq = Res()
            sd = SB(es, "sd", [128, 512], F32); Rsd = Res()
            rs = SB(es, "rs", [128, 512], F32); Rrs = Res()
            onb = [SB(es, f"onb{i}", [128, 512], BF16) for i in range(2)]; Ronb = [Res(), Res()]; don = [k.dsem("doa0"), k.dsem("doa1")]
            psg = [PS(es, f"psg{i}", [128, 512], F32) for i in range(3)]; Rpsg = [Res() for _ in range(3)]
            pS = [PS(es, f"pS{i}", [128, 512], F32) for i in range(2)]; RpS = [Res(), Res()]
            psos = [(PS(es, f"pso{i}", [128, 512], F32), Res()) for i in range(2)]
            psf = PS(es, "psf", [128, 512], F32); Rpsf = Res()
            gcnt = 0; scnt = 0; pcnt = 0; ocnt = 0
            dpre = [k.dsem(f"dpre{i}") for i in range(4)]
            print("phase2 sbuf bytes remaining", nc.sbuf_bytes_remaining)
            def prepass():
                cc = 0
                for ch in range(64):
                    for which, tab in enumerate((u_tab, v_tab)):
                        k.op('pool', lambda: P.dma_start(out=uv16[ch * 256:(ch + 1) * 256, which * D:(which + 1) * D], in_=tab[ch * 256:(ch + 1) * 256, :]),
                             dsem=dpre[cc % 4])
                        cc += 1
                        yield
            pre_gen = prepass() if phases >= 5 else iter(())
            n_iter_total = 8 * sum(NKTP + 4 * qb + 4 for qb in range(NQB))
            pre_every = max(1, n_iter_total // 136)
            it_cnt = 0

            def kv_gen(h):
                nonlocal gcnt
                Kh, RKh = Khs[h % 2]; Vh, RVh = Vhs[h % 2]
                for n in range(NK // 512):
                    g = gcnt % 3; gcnt += 1
                    def mm():
                        ins = None
                        for c in range(2):
                            ins = T.matmul(psg[g][:], wkv[:, c, h * 256:h * 256 + 128], ckv[:, c, n * 512:(n + 1) * 512], start=(c == 0), stop=(c == 1))
                        return ins
                    k.op('pe', mm, reads=[Rckv] + Rwkv, writes=[Rpsg[g]])
                    if n % 2 == 0: k.op('act', lambda: A.copy(out=Kh[:, n * 512:(n + 1) * 512], in_=psg[g][:]), reads=[Rpsg[g]], writes=[RKh])
                    else: k.op('dve', lambda: V.tensor_copy(out=Kh[:, n * 512:(n + 1) * 512], in_=psg[g][:]), reads=[Rpsg[g]], writes=[RKh])
                    yield
                for n in range(NKT // 4):
                    g = gcnt % 3; gcnt += 1
                    def mm():
                        ins = None
                        for i in range(4):
                            kt = n * 4 + i
                            for c in range(2):
                                ins = T.matmul(psg[g][:, i * 128:(i + 1) * 128], ckv[:, c, kt * 128:(kt + 1) * 128],
                                               wkv[:, c, h * 256 + 128:h * 256 + 256], start=(c == 0), stop=(c == 1))
                        return ins
                    k.op('pe', mm, reads=[Rckv] + Rwkv, writes=[Rpsg[g]])
                    o = Vh[:, n * 4:(n + 1) * 4, :]
                    i_ = psg[g][:].rearrange("p (a b) -> p a b", a=4)
                    if n % 2 == 0: k.op('act', lambda: A.copy(out=o, in_=i_), reads=[Rpsg[g]], writes=[RVh])
                    else: k.op('dve', lambda: V.tensor_copy(out=o, in_=i_), reads=[Rpsg[g]], writes=[RVh])
                    yield

            def q_gen(h, qb, qi):
                nonlocal gcnt
                q0 = qb * 512
                qn, Rqn = qns[qi]; qr, Rqr = qrs[qi]; csq, snq, Rcsq, dcsq = csqs[qi]
                k.op('sp', lambda: SP.dma_start(out=csq[:], in_=cs_s[:, q0:q0 + 512]), writes=[Rcsq], dsem=dcsq)
                k.op('sp', lambda: SP.dma_start(out=snq[:], in_=sn_s[:, q0:q0 + 512]), writes=[Rcsq], dsem=dcsq)
                ga, gb_, gc = [(gcnt + i) % 3 for i in range(3)]; gcnt += 3
                def mmq(ps, col, m):
                    ins = None
                    for c in range(4):
                        ins = T.matmul(ps[0:m, :], wq[:, c, col:col + m], cqn[:, c, q0:q0 + 512], start=(c == 0), stop=(c == 3))
                    return ins
                k.op('pe', lambda: mmq(psg[ga], h * 192, 128), reads=[Rcqn] + Rwq, writes=[Rpsg[ga]])
                yield
                k.op('pe', lambda: mmq(psg[gb_], h * 192 + 128, 64), reads=[Rcqn] + Rwq, writes=[Rpsg[gb_]])
                k.op('pe', lambda: mmq(psg[gc], 1536 + h * 64, 64), reads=[Rcqn] + Rwq, writes=[Rpsg[gc]])
                k.op('act', lambda: A.activation(out=qn[:], in_=psg[ga][:], func=AF.Copy, scale=SCALE), reads=[Rpsg[ga]], writes=[Rqn])
                yield
                k.op('dve', lambda: V.scalar_tensor_tensor(out=t1[:], in0=psg[gb_][0:64, :], scalar=SCALE, in1=csq[:], op0=ALU.mult, op1=ALU.mult),
                     reads=[Rpsg[gb_], Rcsq], writes=[Rt1])
                k.op('dve', lambda: V.scalar_tensor_tensor(out=t2[:], in0=psg[gc][0:64, :], scalar=SCALE, in1=snq[:], op0=ALU.mult, op1=ALU.mult),
                     reads=[Rpsg[gc], Rcsq], writes=[Rt2])
                yield
                k.op('dve', lambda: V.tensor_tensor(out=qr[0:64, :], in0=t1[:], in1=t2[:], op=ALU.add), reads=[Rt1, Rt2], writes=[Rqr])
                yield

            def finalize(h, qb, oi):
                nonlocal ocnt
                q0 = qb * 512
                pso, Rpso = psos[oi]; dacc, Rdacc = daccs[oi]
                yield
                yield
                k.op('pe', lambda: T.matmul(psf[:], ones32[:], dacc[:], start=True, stop=True), reads=[Rdacc, Rconst], writes=[Rpsf])
                yield
                k.op('dve', lambda: V.reciprocal(out=rden[:], in_=psf[:]), reads=[Rpsf], writes=[Rrden])
                k.op('dve', lambda: V.tensor_tensor(out=ob[:], in0=pso[:], in1=rden[:], op=ALU.mult), reads=[Rpso, Rrden], writes=[Rob])
                k.op('act', lambda: A.activation(out=sq[:], in_=ob[:], func=AF.Square), reads=[Rob], writes=[Rsq])
                yield
                yield
                yield
                k.op('pe', lambda: T.matmul(psf[:], ones32[:], sq[:], start=True, stop=True), reads=[Rsq, Rconst], writes=[Rpsf])
                yield
                rstd_from_ss(None, psf[:], 128, rs[:], sd[:], Rpsf, Rsd, Rrs)
                on, Ron, dn = onb[ocnt % 2], Ronb[ocnt % 2], don[ocnt % 2]; ocnt += 1
                k.op('dve', lambda: V.scalar_tensor_tensor(out=on[:], in0=ob[:], scalar=cst[:, C_GAO + h:C_GAO + h + 1], in1=rs[:],
                                                          op0=ALU.mult, op1=ALU.mult), reads=[Rob, Rrs, Rcst], writes=[Ron])
                k.op('sp', lambda: SP.dma_start(out=mix_s[8 + h, :, q0:q0 + 512], in_=on[:]), reads=[Ron], dsem=dn)
                yield

            def run_all(g):
                for _ in g: pass
            seq = [(h, qb) for h in range(8) for qb in range(NQB)]
            run_all(kv_gen(0)); run_all(q_gen(0, 0, 0))
            fin_gen = iter(())
            for idx, (h, qb) in enumerate(seq):
                Kh, RKh = Khs[h % 2]; Vh, RVh = Vhs[h % 2]
                qn, Rqn = qns[idx % 2]; qr, Rqr = qrs[idx % 2]
                pso, Rpso = psos[idx % 2]; dacc, Rdacc = daccs[idx % 2]
                nxt = seq[idx + 1] if idx + 1 < len(seq) else None
                kvg = kv_gen(nxt[0]) if (nxt is not None and nxt[0] != h) else iter(())
                qg = q_gen(nxt[0], nxt[1], (idx + 1) % 2) if nxt is not None else iter(())
                nkt = NKTP + 4 * qb + 4
                def emit_pv(kt, c0, pi_):
                    k.op('pe', lambda: T.matmul(pso[:, c0:512], Vh[:, kt, :], pT[pi_][:, c0:512], start=(kt == 0), stop=(kt == nkt - 1)),
                         reads=[RVh, RpT[pi_]], writes=[Rpso])
                    if kt == 0:
                        k.op('dve', lambda: V.tensor_copy(out=dacc[:], in_=pT[pi_][:]), reads=[RpT[pi_]], writes=[Rdacc])
                    else:
                        k.op('dve', lambda: V.tensor_tensor(out=dacc[:, c0:512], in0=dacc[:, c0:512], in1=pT[pi_][:, c0:512], op=ALU.add),
                             reads=[RpT[pi_], Rdacc], writes=[Rdacc])
                pend = None
                kv_done = False
                for kt in range(nkt):
                    j = kt - (NKTP + 4 * qb)
                    c0 = 0 if j <= 0 else j * 128
                    s = scnt % 2; scnt += 1
                    pi_ = pcnt % 3; pcnt += 1
                    def mms():
                        T.matmul(pS[s][:, c0:512], Kh[:, kt * 128:(kt + 1) * 128], qn[:, c0:512], start=True, stop=False)
                        return T.matmul(pS[s][:, c0:512], krT[0:65, kt * 128:(kt + 1) * 128], qr[0:65, c0:512], start=False, stop=True)
                    k.op('pe', mms, reads=[RKh, RkrT, Rqn, Rqr], writes=[RpS[s]])
                    k.op('act', lambda: A.activation(out=pT[pi_][:, c0:512], in_=pS[s][:, c0:512], func=AF.Exp), reads=[RpS[s]], writes=[RpT[pi_]])
                    if j >= 0:
                        k.op('pool', lambda: P.tensor_tensor(out=pT[pi_][:, c0:c0 + 128], in0=pT[pi_][:, c0:c0 + 128], in1=tri[:], op=ALU.mult),
                             reads=[RpT[pi_], Rconst], writes=[RpT[pi_]])
                    if pend is not None: emit_pv(*pend)
                    pend = (kt, c0, pi_)
                    it_cnt += 1
                    if it_cnt % pre_every == 0: next(pre_gen, None)
                    next(fin_gen, None)
                    if kt >= 2 and not kv_done:
                        if next(kvg, 'done') == 'done': kv_done = True
                    if kv_done and kt >= nkt - 6: next(qg, None)
                emit_pv(*pend)
                run_all(fin_gen); run_all(kvg); run_all(qg)
                fin_gen = finalize(h, qb, idx % 2)
            run_all(fin_gen)
            for _ in pre_gen: pass
            k.barrier()
        if phases < 4:
            k.op('sp', lambda: SP.dma_start(out=out_d[0:128, :], in_=xo[0:128, :]), dsem=d_c)
            SP.wait_ge(d_c.sem, d_c.val)
            return nc
        with ExitStack() as es:
            alloc_wst(es)
            wo = SB(es, "wo", [128, NCH, D], BF16); Rwo = [Res(), Res(), Res()]
            load_w16("wo", wo16, NCH, D, wo, Rwo)
            mT = [SB(es, f"mT{i}", [128, NCH, 512], BF16) for i in range(2)]; RmT = [Res(), Res()]; dmT = [k.dsem("dmT0"), k.dsem("dmT1")]
            xst = [(SB(es, f"x3_{i}", [128, D], F32), Res(), k.dsem(f"dx{i}")) for i in range(2)]
            h1 = [(SB(es, f"h1_{i}", [128, D], F32), Res(), k.dsem(f"dh{i}")) for i in range(2)]
            psw = [PS(es, f"psw{i}", [128, 512], F32) for i in range(4)]; Rpsw = [Res() for _ in range(4)]
            tcnt = 0; pcnt = 0
            for blk in range(NBH):
                m, Rm, dm = mT[blk % 2], RmT[blk % 2], dmT[blk % 2]
                for half in range(2):
                    k.op('sp', lambda: SP.dma_start(out=m[:, half * 8:(half + 1) * 8, :],
                                                    in_=mix_s[half * 8:(half + 1) * 8, :, blk * 512:(blk + 1) * 512].rearrange("g p t -> p g t")),
                         writes=[Rm], dsem=dm)
                for tt in range(4):
                    xt, Rx, dx = xst[tcnt % 2]; ht, Rh, dh = h1[tcnt % 2]; tcnt += 1
                    r0 = blk * 512 + tt * 128
                    k.op('sp', lambda: SP.dma_start(out=xt[:], in_=xo[r0:r0 + 128, :]), writes=[Rx], dsem=dx)
                    for cq in range(4):
                        pi_ = pcnt % 4; pcnt += 1
                        def mm():
                            ins = None
                            for c in range(NCH):
                                ins = T.matmul(psw[pi_][:], m[:, c, tt * 128:(tt + 1) * 128], wo[:, c, cq * 512:(cq + 1) * 512],
                                               start=(c == 0), stop=(c == NCH - 1))
                            return ins
                        k.op('pe', mm, reads=[Rm] + Rwo, writes=[Rpsw[pi_]])
                        k.op('dve', lambda: V.tensor_tensor(out=ht[:, cq * 512:(cq + 1) * 512], in0=psw[pi_][:], in1=xt[:, cq * 512:(cq + 1) * 512], op=ALU.add),
                             reads=[Rpsw[pi_], Rx], writes=[Rh])
                    k.op('pool', lambda: P.dma_start(out=h_s[r0:r0 + 128, :], in_=ht[:]), reads=[Rh], dsem=dh)
            k.barrier()
        if phases < 5:
            k.op('sp', lambda: SP.dma_start(out=out_d[0:128, :], in_=xo[0:128, :]), dsem=d_c)
            SP.wait_ge(d_c.sem, d_c.val)
            return nc
        NT = TP // 128
        with ExitStack() as es:
            wpq = SB(es, "wpq", [128, NCH, D], BF16); Rwpq = [Res(), Res(), Res()]
            skb = SB(es, "skb", [128, D], BF16); Rskb = [Res(), Res(), Res()]
            gffn = SB(es, "gffn", [128, D], F32); Rgffn = Res(); dg = k.dsem("dg")
            es_w4 = ExitStack(); alloc_wst(es_w4)
            load_w16("wpq", wpq16, NCH, D, wpq, Rwpq)
            load_w16("sk", sk16, 1, D, skb[:].rearrange("p (o n) -> p o n", o=1), Rskb)
            k.op('sp', lambda: SP.dma_start(out=gffn[:], in_=gffn_d[:, :]), writes=[Rgffn], dsem=dg)
            k.barrier(); es_w4.close()
            NBUF = 7
            gb = [(SB(es, f"gb{i}", [128, 2 * D], BF16), Res(), k.dsem(f"dgb{i}")) for i in range(NBUF)]
            hts = [(SB(es, f"ht{i}", [128, D], F32), Res(), k.dsem(f"dh{i}")) for i in range(2)]; dst_ = k.dsem("dhst")
            fbs = [(SB(es, f"fb{i}", [128, D], BF16), Res()) for i in range(2)]
            fT = SB(es, "fT", [128, NCH, 128], BF16); RfT = Res()
            qT = SB(es, "qT", [128, 16, 128], BF16); RqT = Res()
            sc = SB(es, "sc", [128, 16, 128], F32); Rsc = Res()
            tmp = SB(es, "tmp", [128, 256], F32); Rtmp = Res()
            tv = SB(es, "tv", [128, 16, 16], F32); Rtv = Res()
            ti = SB(es, "ti", [128, 16, 16], U32); Rti = Res()
            tif = SB(es, "tif", [128, 16, 16], F32); Rtif = Res()
            cand = SB(es, "cand", [128, 8, 256], F32); Rcand = Res()
            bv = SB(es, "bv", [128, 8, 16], F32); Rbv = Res()
            bp = SB(es, "bp", [128, 8, 16], U32); Rbp = Res()
            bpf = SB(es, "bpf", [128, 8, 16], F32); Rbpf = Res()
            ai = SB(es, "ai", [128, 8, 16], I32); Rai = Res()
            af = SB(es, "af", [128, 8, 16], F32); Raf = Res()
            bf_ = SB(es, "bf_", [128, 8, 16], F32); Rbf = Res()
            eqa = SB(es, "eqa", [128, 8, 16, 16], BF16); Reqa = Res()
            prod = SB(es, "prod", [128, 8, 16, 16], BF16); Rprod = Res()
            i1 = SB(es, "i1", [128, 8, 16], F32); Ri1 = Res()
            i2 = SB(es, "i2", [128, 8, 16], F32); Ri2 = Res()
            eidf = SB(es, "eidf", [128, 128], F32); Reidf = Res()
            eids = [(SB(es, f"eid{i}", [128, 128], I32), Res()) for i in range(2)]
            gts = [(SB(es, f"gt{i}", [128, 8, 16], F32), Res()) for i in range(2)]
            zz = SB(es, "zz", [128, 16], F32); Rzz = Res()
            hdn = SB(es, "hdn", [128, 128], F32); Rhdn = [Res() for _ in range(128)]
            wv = SB(es, "wv", [128, 128], F32); Rwv = [Res() for _ in range(128)]
            junk = SB(es, "junk", [128, D], BF16); Rjunk = Res()
            dg_ = [(SB(es, f"dg{i}", [128, 128], BF16), Res()) for i in range(3)]
            st4 = SB(es, "st4", [128, 4], F32); Rst4 = Res()
            psT = PS(es, "psT4", [128, D], BF16); RpsT = Res()
            psX = [PS(es, f"psX{i}", [128, 512], F32) for i in range(2)]; RpsX = [Res(), Res()]
            pv = PS(es, "pv", [128, D], F32); Rpv = Res()
            IOTA = 96
            gcnt = 0; xcnt = 0; dcnt = 0
            print("phase4 sbuf bytes remaining", nc.sbuf_bytes_remaining)
            def prep(tt, b):
                nonlocal xcnt
                r0 = tt * 128
                ht, Rht, dht = hts[b]; fb, Rfb = fbs[b]; eid, Reid = eids[b]; gt, Rgt = gts[b]
                k.op('sp', lambda: SP.dma_start(out=ht[:], in_=h_s[r0:r0 + 128, :]), writes=[Rht], dsem=dht)
                k.op('act', lambda: A.activation(out=fb[:], in_=ht[:], func=AF.Square, accum_out=st4[:, 0:1]), reads=[Rht], writes=[Rfb, Rst4])
                rstd_from_ss(None, st4[:, 0:1], D, st4[:, 2:3], st4[:, 1:2], Rst4, Rst4, Rst4)
                k.op('dve', lambda: V.scalar_tensor_tensor(out=fb[:], in0=ht[:], scalar=st4[:, 2:3], in1=gffn[:], op0=ALU.mult, op1=ALU.mult),
                     reads=[Rht, Rst4, Rgffn], writes=[Rfb])
                yield
                def tr():
                    ins = None
                    for c in range(NCH):
                        ins = T.transpose(psT[:, c * 128:(c + 1) * 128], fb[:, c * 128:(c + 1) * 128], identb[:])
                    return ins
                k.op('pe', tr, reads=[Rfb, Rconst], writes=[RpsT])
                k.op('act', lambda: A.copy(out=fT[:].rearrange("p c t -> p (c t)"), in_=psT[:]), reads=[RpsT], writes=[RfT])
                yield
                for q4 in range(4):
                    pq = psX[xcnt % 2]; Rpq = RpsX[xcnt % 2]; xcnt += 1
                    def mm():
                        ins = None
                        for i in range(4):
                            hp = q4 * 4 + i
                            for c in range(NCH):
                                ins = T.matmul(pq[:, i * 128:(i + 1) * 128], wpq[:, c, hp * 128:(hp + 1) * 128], fT[:, c, :],
                                               start=(c == 0), stop=(c == NCH - 1))
                        return ins
                    k.op('pe', mm, reads=[RfT] + Rwpq, writes=[Rpq])
                    o = qT[:, q4 * 4:(q4 + 1) * 4, :].rearrange("p a t -> p (a t)")
                    k.op('act', lambda: A.copy(out=o, in_=pq[:]), reads=[Rpq], writes=[RqT])
                    yield
                yield
                for q4 in range(4):
                    pq = psX[xcnt % 2]; Rpq = RpsX[xcnt % 2]; xcnt += 1
                    def mm():
                        ins = None
                        for i in range(4):
                            hp = q4 * 4 + i
                            ins = T.matmul(pq[:, i * 128:(i + 1) * 128], qT[:, hp, :], skb[:, hp * 128:(hp + 1) * 128], start=True, stop=True)
                        return ins
                    k.op('pe', mm, reads=[RqT] + Rskb, writes=[Rpq])
                    o = sc[:, q4 * 4:(q4 + 1) * 4, :].rearrange("p a t -> p (a t)")
                    k.op('act', lambda: A.copy(out=o, in_=pq[:]), reads=[Rpq], writes=[Rsc])
                    yield
                for hp in range(16):
                    k.op('dve', lambda: V.max(out=tv[:, hp, 0:8], in_=sc[:, hp, :]), reads=[Rsc], writes=[Rtv])
                    k.op('dve', lambda: V.max_index(out=ti[:, hp, 0:8], in_max=tv[:, hp, 0:8], in_values=sc[:, hp, :]), reads=[Rsc, Rtv], writes=[Rti])
                    k.op('dve', lambda: V.match_replace(out=tmp[:, 0:128], in_to_replace=tv[:, hp, 0:8], in_values=sc[:, hp, :], imm_value=-1e30),
                         reads=[Rsc, Rtv], writes=[Rtmp])
                    k.op('dve', lambda: V.max(out=tv[:, hp, 8:16], in_=tmp[:, 0:128]), reads=[Rtmp], writes=[Rtv])
                    k.op('dve', lambda: V.max_index(out=ti[:, hp, 8:16], in_max=tv[:, hp, 8:16], in_values=tmp[:, 0:128]), reads=[Rtmp, Rtv], writes=[Rti])
                    if hp % 2 == 1: yield
                k.op('dve', lambda: V.tensor_copy(out=tif[:], in_=ti[:]), reads=[Rti], writes=[Rtif])
                tvv = tv[:].rearrange("p (h two) k -> p h two k", two=2)
                tfv = tif[:].rearrange("p (h two) k -> p h two k", two=2)
                k.op('dve', lambda: V.tensor_tensor(out=cand[:].rearrange("p h (a b) -> p h a b", a=16),
                                                    in0=tvv[:, :, 0, :].unsqueeze(3).to_broadcast([128, 8, 16, 16]),
                                                    in1=tvv[:, :, 1, :].unsqueeze(2).to_broadcast([128, 8, 16, 16]), op=ALU.add),
                     reads=[Rtv], writes=[Rcand])
                for h in range(8):
                    k.op('dve', lambda: V.max(out=bv[:, h, 0:8], in_=cand[:, h, :]), reads=[Rcand], writes=[Rbv])
                    k.op('dve', lambda: V.max_index(out=bp[:, h, 0:8], in_max=bv[:, h, 0:8], in_values=cand[:, h, :]), reads=[Rcand, Rbv], writes=[Rbp])
                    k.op('dve', lambda: V.match_replace(out=tmp[:], in_to_replace=bv[:, h, 0:8], in_values=cand[:, h, :], imm_value=-1e30),
                         reads=[Rcand, Rbv], writes=[Rtmp])
                    k.op('dve', lambda: V.max(out=bv[:, h, 8:16], in_=tmp[:]), reads=[Rtmp], writes=[Rbv])
                    k.op('dve', lambda: V.max_index(out=bp[:, h, 8:16], in_max=bv[:, h, 8:16], in_values=tmp[:]), reads=[Rtmp, Rbv], writes=[Rbp])
                    if h % 2 == 1: yield
                k.op('dve', lambda: V.tensor_copy(out=bpf[:], in_=bp[:]), reads=[Rbp], writes=[Rbpf])
                k.op('dve', lambda: V.tensor_scalar(out=ai[:], in0=bpf[:], scalar1=0.0625, scalar2=-0.46875, op0=ALU.mult, op1=ALU.add),
                     reads=[Rbpf], writes=[Rai])
                k.op('dve', lambda: V.tensor_copy(out=af[:], in_=ai[:]), reads=[Rai], writes=[Raf])
                k.op('dve', lambda: V.scalar_tensor_tensor(out=bf_[:], in0=af[:], scalar=-16.0, in1=bpf[:], op0=ALU.mult, op1=ALU.add),
                     reads=[Raf, Rbpf], writes=[Rbf])
                iot = cst[:, IOTA:IOTA + 16].unsqueeze(1).unsqueeze(1).to_broadcast([128, 8, 16, 16])
                for (src, which, dsti, Rd) in ((af, 0, i1, Ri1), (bf_, 1, i2, Ri2)):
                    k.op('dve', lambda: V.tensor_tensor(out=eqa[:], in0=src[:].unsqueeze(3).to_broadcast([128, 8, 16, 16]), in1=iot, op=ALU.is_equal),
                         reads=[Raf, Rbf, Rcst], writes=[Reqa])
                    k.op('dve', lambda: V.tensor_tensor(out=prod[:], in0=eqa[:], in1=tfv[:, :, which, :].unsqueeze(2).to_broadcast([128, 8, 16, 16]), op=ALU.mult),
                         reads=[Reqa, Rtif], writes=[Rprod])
                    k.op('dve', lambda: V.tensor_reduce(out=dsti[:], in_=prod[:], axis=AX.X, op=ALU.add), reads=[Rprod], writes=[Rd])
                k.op('dve', lambda: V.scalar_tensor_tensor(out=eidf[:], in0=i1[:].rearrange("p h k -> p (h k)"), scalar=128.0,
                                                          in1=i2[:].rearrange("p h k -> p (h k)"), op0=ALU.mult, op1=ALU.add),
                     reads=[Ri1, Ri2], writes=[Reidf])
                k.op('dve', lambda: V.tensor_copy(out=eid[:], in_=eidf[:]), reads=[Reidf], writes=[Reid])
                yield
                k.op('dve', lambda: V.tensor_tensor(out=gt[:], in0=bv[:], in1=bv[:, :, 0:1].to_broadcast([128, 8, 16]), op=ALU.subtract),
                     reads=[Rbv], writes=[Rgt])
                k.op('act', lambda: A.activation(out=gt[:], in_=gt[:], func=AF.Exp), reads=[Rgt], writes=[Rgt])
                k.op('dve', lambda: V.tensor_reduce(out=zz[:, 0:8], in_=gt[:], axis=AX.X, op=ALU.add), reads=[Rgt], writes=[Rzz])
                k.op('dve', lambda: V.reciprocal(out=zz[:, 8:16], in_=zz[:, 0:8]), reads=[Rzz], writes=[Rzz])
                k.op('dve', lambda: V.tensor_tensor(out=gt[:], in0=gt[:], in1=zz[:, 8:16].unsqueeze(2).to_broadcast([128, 8, 16]), op=ALU.mult),
                     reads=[Rgt, Rzz], writes=[Rgt])
            def run_all(g):
                for _ in g: pass
            run_all(prep(0, 0))
            for tt in range(NT):
                r0 = tt * 128
                b = tt % 2
                ht, Rht, dht = hts[b]; fb, Rfb = fbs[b]; eid, Reid = eids[b]; gt, Rgt = gts[b]
                gtf = gt[:].rearrange("p h k -> p (h k)")
                nxt = prep(tt + 1, 1 - b) if tt + 1 < NT else iter(())
                for j in range(128):
                    g_, Rg, dgb = gb[gcnt % NBUF]; gcnt += 1
                    dgt, Rdg = dg_[dcnt % 3]; dcnt += 1
                    k.op('pool', lambda: P.indirect_dma_start(out=g_[:], out_offset=None, in_=uv16[:, :],
                                                             in_offset=bass.IndirectOffsetOnAxis(ap=eid[:, j:j + 1], axis=0)),
                         reads=[Reid], writes=[Rg], dsem=dgb)
                    k.op('dve', lambda: V.scalar_tensor_tensor(out=junk[:], in0=g_[:, 0:D], scalar=1.0, in1=fb[:], op0=ALU.mult, op1=ALU.mult,
                                                              accum_out=hdn[:, j:j + 1]), reads=[Rg, Rfb], writes=[Rjunk, Rhdn[j]])
                    k.op('act', lambda: A.activation(out=wv[:, j:j + 1], in_=hdn[:, j:j + 1], func=AF.Gelu), reads=[Rhdn[j]], writes=[Rwv[j]])
                    k.op('act', lambda: A.mul(out=wv[:, j:j + 1], in_=wv[:, j:j + 1], mul=gtf[:, j:j + 1]), reads=[Rwv[j], Rgt], writes=[Rwv[j]])
                    k.op('act', lambda: A.activation(out=dgt[:], in_=identb[:], func=AF.Copy, scale=wv[:, j:j + 1]), reads=[Rwv[j], Rconst], writes=[Rdg])
                    def mmv():
                        ins = None
                        for q in range(4):
                            ins = T.matmul(pv[:, q * 512:(q + 1) * 512], dgt[:], g_[:, D + q * 512:D + (q + 1) * 512], start=(j == 0), stop=(j == 127))
                        return ins
                    k.op('pe', mmv, reads=[Rdg, Rg], writes=[Rpv])
                    if j % 4 == 3: next(nxt, None)
                run_all(nxt)
                k.op('dve', lambda: V.tensor_tensor(out=ht[:], in0=pv[:], in1=ht[:], op=ALU.add), reads=[Rpv, Rht], writes=[Rht])
                k.op('sp', lambda: SP.dma_start(out=h_s[r0:r0 + 128, :], in_=ht[:]), reads=[Rht], dsem=dst_)
            k.barrier()
        if phases < 6:
            k.op('sp', lambda: SP.dma_start(out=out_d[0:128, :], in_=xo[0:128, :]), dsem=d_c)
            SP.wait_ge(d_c.sem, d_c.val)
            return nc
        with ExitStack() as es:
            alloc_wst(es)
            wg = SB(es, "wg", [128, NCH, D], BF16); Rwg = [Res(), Res(), Res()]
            load_w16("wpg", wpg16, NCH, D, wg, Rwg)
            wp = SB(es, "wp", [128, 2, D], BF16); Rwp = [Res(), Res(), Res()]
            load_w16("wpp", wpp16, 2, D, wp, Rwp)
            gfin = SB(es, "gfin", [128, D], F32); Rgfin = Res(); dg = k.dsem("dg")
            k.op('sp', lambda: SP.dma_start(out=gfin[:], in_=gfin_d[:, :]), writes=[Rgfin], dsem=dg)
            hts = [(SB(es, f"ht5_{i}", [128, D], F32), Res(), k.dsem(f"dh{i}")) for i in range(2)]
            pts = [(SB(es, f"pt5_{i}", [128, 256], F32), Res(), k.dsem(f"dx{i}")) for i in range(2)]
            hbs = [(SB(es, f"hb{i}", [128, D], BF16), Res()) for i in range(2)]
            pbs = [(SB(es, f"pb{i}", [128, 256], BF16), Res()) for i in range(2)]
            hTs = [(SB(es, f"hT{i}", [128, NCH, 128], BF16), Res()) for i in range(2)]
            pT5s = [(SB(es, f"pT5{i}", [128, 2, 128], BF16), Res()) for i in range(2)]
            sgs = [(SB(es, f"sg{i}", [128, 512], F32), Res()) for i in range(2)]
            h3s = [(SB(es, f"h3{i}", [128, D], F32), Res()) for i in range(2)]
            ots = [(SB(es, f"ot5_{i}", [128, D], F32), Res(), k.dsem(f"dgb{i}")) for i in range(2)]
            st5s = [(SB(es, f"st5{i}", [128, 8], F32), Res()) for i in range(2)]
            psTs = [(PS(es, f"psT5{i}", [128, D], BF16), Res()) for i in range(2)]
            psP = PS(es, "psP5", [128, 256], BF16); RpsP = Res()
            psg5 = [PS(es, f"psg5_{i}", [128, 512], F32) for i in range(2)]; Rpsg5 = [Res(), Res()]
            psp5 = [PS(es, f"psp5_{i}", [128, 512], F32) for i in range(1)]; Rpsp5 = [Res()]
            pc = 0
            def front5(tt):
                r0 = tt * 128
                ht, Rht, dht = hts[tt % 2]; pt, Rpt, dpt = pts[tt % 2]; ot, Rot, dot = ots[tt % 2]
                hb, Rhb = hbs[tt % 2]; pb, Rpb = pbs[tt % 2]; hT, RhT = hTs[tt % 2]; pT5, RpT5 = pT5s[tt % 2]
                h3, Rh3 = h3s[tt % 2]; st5, Rst5 = st5s[tt % 2]; psT, RpsT = psTs[tt % 2]
                k.op('sp', lambda: SP.dma_start(out=ht[:], in_=h_s[r0:r0 + 128, :]), writes=[Rht], dsem=dht)
                k.op('sp', lambda: SP.dma_start(out=pt[:], in_=pp[r0:r0 + 128, :]), writes=[Rpt], dsem=dpt)
                k.op('act', lambda: A.activation(out=hb[:], in_=ht[:], func=AF.Square, accum_out=st5[:, 0:1]), reads=[Rht], writes=[Rhb, Rst5])
                rstd_from_ss(None, st5[:, 0:1], D, st5[:, 2:3], st5[:, 1:2], Rst5, Rst5, Rst5)
                k.op('dve', lambda: V.tensor_scalar(out=hb[:], in0=ht[:], scalar1=st5[:, 2:3], scalar2=None, op0=ALU.mult), reads=[Rht, Rst5], writes=[Rhb])
                k.op('act', lambda: A.copy(out=pb[:], in_=pt[:]), reads=[Rpt], writes=[Rpb])
                def tr():
                    ins = None
                    for c in range(NCH):
                        ins = T.transpose(psT[:, c * 128:(c + 1) * 128], hb[:, c * 128:(c + 1) * 128], identb[:])
                    return ins
                k.op('pe', tr, reads=[Rhb, Rconst], writes=[RpsT])
                def tr2():
                    ins = None
                    for c in range(2):
                        ins = T.transpose(psP[:, c * 128:(c + 1) * 128], pb[:, c * 128:(c + 1) * 128], identb[:])
                    return ins
                k.op('pe', tr2, reads=[Rpb, Rconst], writes=[RpsP])
                k.op('dve', lambda: V.tensor_tensor(out=hT[:], in0=psT[:].rearrange("p (c t) -> p c t", c=NCH),
                                                    in1=cst[:, C_GPLE:C_GPLE + NCH].unsqueeze(2).to_broadcast([128, NCH, 128]), op=ALU.mult),
                     reads=[RpsT, Rcst], writes=[RhT])
                k.op('act', lambda: A.copy(out=pT5[:].rearrange("p c t -> p (c t)"), in_=psP[:]), reads=[RpsP], writes=[RpT5])
            def back5(tt):
                nonlocal pc
                r0 = tt * 128
                ht, Rht, dht = hts[tt % 2]; pt, Rpt, dpt = pts[tt % 2]; ot, Rot, dot = ots[tt % 2]
                hb, Rhb = hbs[tt % 2]; pb, Rpb = pbs[tt % 2]; hT, RhT = hTs[tt % 2]; pT5, RpT5 = pT5s[tt % 2]
                h3, Rh3 = h3s[tt % 2]; st5, Rst5 = st5s[tt % 2]; psT, RpsT = psTs[tt % 2]
                for cq in range(4):
                    pi_ = pc % 2; pc += 1
                    sg, Rsg = sgs[pi_]
                    def mmg():
                        ins = None
                        for c in range(NCH):
                            ins = T.matmul(psg5[pi_][:], hT[:, c, :], wg[:, c, cq * 512:(cq + 1) * 512], start=(c == 0), stop=(c == NCH - 1))
                        return ins
                    def mmp():
                        ins = None
                        for c in range(2):
                            ins = T.matmul(psp5[0][:], pT5[:, c, :], wp[:, c, cq * 512:(cq + 1) * 512], start=(c == 0), stop=(c == 1))
                        return ins
                    k.op('pe', mmg, reads=[RhT] + Rwg, writes=[Rpsg5[pi_]])
                    k.op('pe', mmp, reads=[RpT5] + Rwp, writes=[Rpsp5[0]])
                    k.op('act', lambda: A.activation(out=sg[:], in_=psg5[pi_][:], func=AF.Sigmoid), reads=[Rpsg5[pi_]], writes=[Rsg])
                    k.op('dve', lambda: V.tensor_tensor(out=sg[:], in0=psp5[0][:], in1=sg[:], op=ALU.mult), reads=[Rpsp5[0], Rsg], writes=[Rsg])
                    k.op('dve', lambda: V.tensor_tensor(out=h3[:, cq * 512:(cq + 1) * 512], in0=sg[:], in1=ht[:, cq * 512:(cq + 1) * 512], op=ALU.add),
                         reads=[Rsg, Rht], writes=[Rh3])
                k.op('act', lambda: A.activation(out=hb[:], in_=h3[:], func=AF.Square, accum_out=st5[:, 4:5]), reads=[Rh3], writes=[Rhb, Rst5])
                rstd_from_ss(None, st5[:, 4:5], D, st5[:, 6:7], st5[:, 5:6], Rst5, Rst5, Rst5)
                k.op('dve', lambda: V.scalar_tensor_tensor(out=ot[:], in0=h3[:], scalar=st5[:, 6:7], in1=gfin[:], op0=ALU.mult, op1=ALU.mult),
                     reads=[Rh3, Rst5, Rgfin], writes=[Rot])
                k.op('pool', lambda: P.dma_start(out=out_d[r0:r0 + 128, :], in_=ot[:]), reads=[Rot], dsem=dot)
            front5(0)
            for tt in range(NT):
                if tt + 1 < NT: front5(tt + 1)
                back5(tt)
            k.barrier()
    return nc


def prep(inp, NBH):
    TP = NBH * 512
    f32 = np.float32
    x = np.asarray(inp['x'], f32); p = np.asarray(inp['p'], f32); pos = np.asarray(inp['positions'], np.int32)
    def fm(g, n): return np.ascontiguousarray(np.asarray(g, f32).reshape(n, 128).T)
    cst = np.zeros((128, 128), f32)
    invf = (np.float32(10000.0) ** (-(np.arange(0, 64, 2, dtype=f32) / np.float32(64)))).astype(f32)
    cst[0:64, 0] = np.concatenate([invf, invf]); cst[0:32, 1] = -1.0; cst[32:64, 1] = 1.0
    cst[:, 2:18] = fm(inp['attn_norm'][0], 16); cst[:, 18:22] = fm(inp['q_norm'][0], 4); cst[:, 22:24] = fm(inp['kv_norm'][0], 2)
    cst[:, 24:32] = fm(inp['conv_out_norm'][0], 8); cst[:, 32:40] = fm(inp['attn_out_norm'][0], 8)
    cst[:, 40:56] = fm(inp['ffn_norm'][0], 16); cst[:, 56:72] = fm(inp['ple_norm'][0], 16)
    cw = np.asarray(inp['conv_w'][0], f32)
    cst[:, 72:96] = cw.reshape(3, 8, 128).transpose(2, 1, 0).reshape(128, 24)
    cst[:, 96:112] = np.arange(16, dtype=f32)[None]
    w_in = np.ascontiguousarray(np.asarray(inp['w_in'][0], f32))
    w_in_sw = np.ascontiguousarray(np.concatenate([w_in[:, 3872:3904], w_in[:, 3840:3872]], axis=1))
    w_uq = np.ascontiguousarray(np.asarray(inp['w_uq'][0], f32))
    wr = w_uq.reshape(512, 8, 192)
    w_uq_sw = np.ascontiguousarray(np.concatenate([wr[:, :, 160:192], wr[:, :, 128:160]], axis=2).reshape(512, 512))
    sk = np.asarray(inp['sub_keys'][0], f32)
    skT = np.ascontiguousarray(sk.transpose(3, 0, 1, 2).reshape(128, 2048))
    shared = dict(cst=cst, ident=np.eye(128, dtype=f32), w_in=w_in, w_in_sw=w_in_sw, w_uq=w_uq, w_uq_sw=w_uq_sw,
                  w_ukv=np.ascontiguousarray(np.asarray(inp['w_ukv'][0], f32)), w_out=np.ascontiguousarray(np.asarray(inp['w_out'][0], f32)),
                  w_pq=np.ascontiguousarray(np.asarray(inp['w_pq'][0], f32)), w_pg=np.ascontiguousarray(np.asarray(inp['w_ple_gate'][0], f32)),
                  w_pp=np.ascontiguousarray(np.asarray(inp['w_ple_proj'][0], f32)), skT=skT,
                  gffn_rep=np.ascontiguousarray(np.tile(np.asarray(inp['ffn_norm'][0], f32)[None], (128, 1))),
                  gfin_rep=np.ascontiguousarray(np.tile(np.asarray(inp['final_norm'], f32)[None], (128, 1))),
                  u_tab=np.ascontiguousarray(np.asarray(inp['u_tab'][0], f32)), v_tab=np.ascontiguousarray(np.asarray(inp['v_tab'][0], f32)))
    maps = []
    for c in range(N_CORES):
        b, half = c // 2, c % 2
        own = slice(half * TP, (half + 1) * TP)
        m = dict(shared)
        m['xo'] = np.ascontiguousarray(x[b, own])
        m['pp'] = np.ascontiguousarray(p[0, b, own])
        kb = np.zeros((1, 2 * TP), f32)
        if half == 1:
            m['xp'] = np.ascontiguousarray(x[b, 0:TP]); ppos = pos[b, 0:TP]
        else:
            m['xp'] = np.zeros((TP, D), f32); ppos = np.zeros((TP,), np.int32); kb[0, 0:TP] = -30000.0
        m['posr'] = np.ascontiguousarray(np.concatenate([ppos, pos[b, own]])[None].astype(np.int32))
        m['kbias'] = kb
        maps.append(m)
    return maps


_NC_CACHE = {}


def kernel(**inputs):
    S = np.asarray(inputs['x']).shape[1]
    NBH = S // 1024
    if NBH not in _NC_CACHE:
        _NC_CACHE[NBH] = build(NBH)
    nc = _NC_CACHE[NBH]
    maps = prep(inputs, NBH)
    res = run_bass_kernel_spmd(nc, maps, core_ids=list(range(N_CORES)))
    B = np.asarray(inputs['x']).shape[0]
    out = np.zeros((B, S, D), np.float32)
    TP = NBH * 512
    for c in range(N_CORES):
        b, half = c // 2, c % 2
        out[b, half * TP:(half + 1) * TP] = res.results[c]['out']
    return out
```
